# Optimizing a Trainium2 kernel written in Bass

```python
import jax, jax.numpy as jnp
from jax import lax
import numpy as np

D_MODEL = 2048
BATCH = 1
SEQ = 16384
DEPTH = 1

GRID_W = 64
CTX_LEN = 256
D_CONV = 2048
CONV_WIDTH = 31
D_RNN = 2048
RNN_BLOCKS = 16
RNN_BW = D_RNN // RNN_BLOCKS
SHORT_CONV = 4
LRU_C = 8.0
EPS = 1e-6

SPLITS = [D_CONV, 2 * D_CONV, 3 * D_CONV, 3 * D_CONV + D_RNN, 3 * D_CONV + 2 * D_RNN,
          3 * D_CONV + 2 * D_RNN + D_MODEL]
D_IN = 3 * D_CONV + 2 * D_RNN + 2 * D_MODEL
RX0 = 3 * D_CONV
RX1 = RX0 + D_RNN

kernel_name = "hybrid_conformer_rglru_dit_block"


def rmsnorm(x, g):
    xf = x.astype(jnp.float32)
    y = xf * lax.rsqrt(jnp.mean(xf * xf, axis=-1, keepdims=True) + EPS)
    return (y * g.astype(jnp.float32)).astype(x.dtype)


def layernorm(x, g, b):
    xf = x.astype(jnp.float32)
    mu = jnp.mean(xf, axis=-1, keepdims=True)
    var = jnp.mean(jnp.square(xf - mu), axis=-1, keepdims=True)
    y = (xf - mu) * lax.rsqrt(var + EPS)
    return (y * g.astype(jnp.float32) + b.astype(jnp.float32)).astype(x.dtype)


def adaln(c_vec, w_ada, b_ada):
    m = jax.nn.silu(c_vec) @ w_ada + b_ada
    return jnp.split(m, 3, axis=-1)


def depthwise_conv(v, k, pad_lo, pad_hi):
    return lax.conv_general_dilated(
        v, k[:, None, :].astype(v.dtype), window_strides=(1,), padding=[(pad_lo, pad_hi)],
        dimension_numbers=("NWC", "WIO", "NWC"), feature_group_count=v.shape[-1])


def conformer_branch(a, g, z, p, grid):
    v = a * jax.nn.sigmoid(g)
    bsz, length, ch = v.shape
    half = CONV_WIDTH // 2
    if grid:
        rows = length // GRID_W
        vr = v.reshape(bsz * rows, GRID_W, ch)
        v = depthwise_conv(vr, p["conv_dw"], half, half).reshape(bsz, length, ch)
    else:
        v = depthwise_conv(v, p["conv_dw"], half, half)
    v = v + p["conv_dw_b"]
    v = jax.nn.silu(layernorm(v, p["conv_ln_g"], p["conv_ln_b"]))
    return (v * jax.nn.silu(z)) @ p["w_conv_out"]


def _lru_combine(left, right):
    a_l, b_l = left
    a_r, b_r = right
    return a_l * a_r, a_r * b_l + b_r


def rglru_direction(xr, p, d, h0, reverse):
    pad = (0, SHORT_CONV - 1) if reverse else (SHORT_CONV - 1, 0)
    xc = depthwise_conv(xr, p["rnn_conv"][d], pad[0], pad[1]) + p["rnn_conv_b"][d]
    bsz, length, _ = xc.shape
    xb = xc.reshape(bsz, length, RNN_BLOCKS, RNN_BW)
    r = jax.nn.sigmoid((jnp.einsum("blhi,hij->blhj", xb, p["rnn_w_r"][d]).reshape(bsz, length, D_RNN)
                        + p["rnn_b_r"][d]).astype(jnp.float32))
    i = jax.nn.sigmoid((jnp.einsum("blhi,hij->blhj", xb, p["rnn_w_i"][d]).reshape(bsz, length, D_RNN)
                        + p["rnn_b_i"][d]).astype(jnp.float32))
    log_a = -LRU_C * r * jax.nn.softplus(-p["rnn_lam"][d].astype(jnp.float32))
    a = jnp.exp(log_a)
    b = jnp.sqrt(-jnp.expm1(2.0 * log_a)) * i * xc.astype(jnp.float32)
    a_cum, b_cum = lax.associative_scan(_lru_combine, (a, b), axis=1, reverse=reverse)
    return b_cum + a_cum * h0[:, None, :]


def mixer(u, p, grid, h0f, h0b):
    a, g, z_c, xr, z_r, gc, gr = jnp.split(u, SPLITS, axis=-1)
    y_conv = conformer_branch(a, g, z_c, p, grid)
    hf = rglru_direction(xr, p, 0, h0f, False)
    hb = rglru_direction(xr, p, 1, h0b, True)
    y_rnn = ((hf + hb).astype(u.dtype) * jax.nn.silu(z_r)) @ p["w_rnn_out"]
    y = jax.nn.sigmoid(gc) * y_conv + jax.nn.sigmoid(gr) * y_rnn
    return y @ p["w_o"], hf[:, -1], hb[:, 0]


def setup_inputs(seed: int = 0) -> dict:
    key = jax.random.key(seed)
    ks = jax.random.split(key, 24)
    f32 = jnp.float32
    nrm = lambda k, shape, s: jax.random.normal(k, shape, f32) * s
    a0 = jax.random.uniform(ks[20], (DEPTH, 2, D_RNN), f32, 0.9, 0.999)
    s0 = a0 ** (1.0 / LRU_C)
    return {
        "x": nrm(ks[0], (BATCH, SEQ, D_MODEL), 1.0),
        "c": nrm(ks[1], (BATCH, D_MODEL), 1.0),
        "ctx": nrm(ks[2], (BATCH, CTX_LEN, D_MODEL), 1.0),
        "c_ctx": nrm(ks[3], (D_MODEL,), 1.0),
        "w_ada": nrm(ks[4], (DEPTH, D_MODEL, 3 * D_MODEL), 0.5 * D_MODEL ** -0.5),
        "b_ada": nrm(ks[5], (DEPTH, 3 * D_MODEL), 0.01),
        "norm_g": 1.0 + nrm(ks[6], (DEPTH, D_MODEL), 0.05),
        "w_in": nrm(ks[7], (DEPTH, D_MODEL, D_IN), D_MODEL ** -0.5),
        "b_in": nrm(ks[8], (DEPTH, D_IN), 0.01),
        "conv_dw": nrm(ks[9], (DEPTH, CONV_WIDTH, D_CONV), CONV_WIDTH ** -0.5),
        "conv_dw_b": nrm(ks[10], (DEPTH, D_CONV), 0.01),
        "conv_ln_g": 1.0 + nrm(ks[11], (DEPTH, D_CONV), 0.05),
        "conv_ln_b": nrm(ks[12], (DEPTH, D_CONV), 0.01),
        "w_conv_out": nrm(ks[13], (DEPTH, D_CONV, D_MODEL), D_CONV ** -0.5),
        "rnn_conv": nrm(ks[14], (DEPTH, 2, SHORT_CONV, D_RNN), SHORT_CONV ** -0.5),
        "rnn_conv_b": nrm(ks[15], (DEPTH, 2, D_RNN), 0.01),
        "rnn_w_r": nrm(ks[16], (DEPTH, 2, RNN_BLOCKS, RNN_BW, RNN_BW), RNN_BW ** -0.5),
        "rnn_b_r": nrm(ks[17], (DEPTH, 2, D_RNN), 0.01),
        "rnn_w_i": nrm(ks[18], (DEPTH, 2, RNN_BLOCKS, RNN_BW, RNN_BW), RNN_BW ** -0.5),
        "rnn_b_i": nrm(ks[19], (DEPTH, 2, D_RNN), 0.01),
        "rnn_lam": jnp.log(s0) - jnp.log1p(-s0),
        "w_rnn_out": nrm(ks[21], (DEPTH, D_RNN, D_MODEL), D_RNN ** -0.5),
        "w_o": nrm(ks[22], (DEPTH, D_MODEL, D_MODEL), D_MODEL ** -0.5),
        "final_g": 1.0 + nrm(ks[23], (D_MODEL,), 0.05),
    }


def reference(x, c, ctx, c_ctx, w_ada, b_ada, norm_g, w_in, b_in, conv_dw, conv_dw_b, conv_ln_g,
              conv_ln_b, w_conv_out, rnn_conv, rnn_conv_b, rnn_w_r, rnn_b_r, rnn_w_i, rnn_b_i,
              rnn_lam, w_rnn_out, w_o, final_g):
    bsz = x.shape[0]
    for l in range(DEPTH):
        p = {
            "conv_dw": conv_dw[l], "conv_dw_b": conv_dw_b[l], "conv_ln_g": conv_ln_g[l],
            "conv_ln_b": conv_ln_b[l], "w_conv_out": w_conv_out[l], "rnn_conv": rnn_conv[l],
            "rnn_conv_b": rnn_conv_b[l], "rnn_w_r": rnn_w_r[l], "rnn_b_r": rnn_b_r[l],
            "rnn_w_i": rnn_w_i[l], "rnn_b_i": rnn_b_i[l], "rnn_lam": rnn_lam[l],
            "w_rnn_out": w_rnn_out[l], "w_o": w_o[l],
        }
        sh, sc, gt = adaln(c, w_ada[l], b_ada[l])
        sh_c, sc_c, gt_c = adaln(c_ctx, w_ada[l], b_ada[l])
        hn = rmsnorm(x, norm_g[l]) * (1.0 + sc[:, None]) + sh[:, None]
        hn_ctx = rmsnorm(ctx, norm_g[l]) * (1.0 + sc_c) + sh_c
        zeros = jnp.zeros((bsz, D_RNN), jnp.float32)
        if l < DEPTH - 1:
            u_ctx = hn_ctx @ w_in[l] + b_in[l]
            y_ctx, hcf, hcb = mixer(u_ctx, p, False, zeros, zeros)
            ctx = ctx + gt_c * y_ctx
        else:
            xr_ctx = hn_ctx @ w_in[l, :, RX0:RX1] + b_in[l, RX0:RX1]
            hcf = rglru_direction(xr_ctx, p, 0, zeros, False)[:, -1]
            hcb = rglru_direction(xr_ctx, p, 1, zeros, True)[:, 0]
        u = hn @ w_in[l] + b_in[l]
        y, _, _ = mixer(u, p, True, hcf, hcb)
        x = x + gt[:, None] * y
    return rmsnorm(x, final_g)
```

```python
from contextlib import ExitStack

import numpy as np
import concourse.bass as bass
import concourse.mybir as mybir
from concourse.bass_utils import run_bass_kernel_spmd

F32 = mybir.dt.float32
BF16 = mybir.dt.bfloat16
AF = mybir.ActivationFunctionType
ALU = mybir.AluOpType
AX = mybir.AxisListType

NCORES = 8
D = 2048
NCH = 16
TOK = 2048
T = 512
NT = TOK // T
CTX = 256
EPS = 1e-6


class Sem:
    def __init__(self, handle, name):
        self.h = handle
        self.name = name
        self.count = 0


class Buf:
    def __init__(self, name):
        self.name = name
        self.last_write = None
        self.reads = {}


class Eng:
    def __init__(self, name, sem):
        self.name = name
        self.sem = sem
        self.waited = {}
        self.prog = []

    def _need(self, pending, tok, same_ok):
        if tok is None:
            return
        sem, val, en = tok
        if same_ok and en == self.name:
            return
        if self.waited.get(sem, 0) >= val:
            return
        pending[sem] = max(pending.get(sem, 0), val)

    def deps(self, reads=(), writes=(), extra=()):
        pending = {}
        for b in reads:
            self._need(pending, b.last_write, False)
        for b in writes:
            self._need(pending, b.last_write, True)
            for rs, (rv, ren) in b.reads.items():
                self._need(pending, (rs, rv, ren), True)
        for tok in extra:
            self._need(pending, tok, False)
        for sem, val in pending.items():
            self.prog.append(("wait", sem, val))
            self.waited[sem] = val

    def _record(self, tok, reads, writes):
        for b in reads:
            old = b.reads.get(tok[0])
            if old is None or old[0] < tok[1]:
                b.reads[tok[0]] = (tok[1], tok[2])
        for b in writes:
            b.last_write = tok
            b.reads = {}

    def op(self, fn, reads=(), writes=(), extra=(), inc=True):
        self.deps(reads, writes, extra)
        if inc:
            self.sem.count += 1
            self.prog.append(("ins", fn, self.sem, 1))
            tok = (self.sem, self.sem.count, self.name)
        else:
            self.prog.append(("ins0", fn))
            tok = (self.sem, self.sem.count + 1, self.name)
        self._record(tok, reads, writes)
        return tok

    def dma(self, dsem, out, in_, reads=(), writes=(), extra=()):
        self.deps(reads, writes, extra)
        dsem.count += 16
        self.prog.append(("ins", lambda e: e.dma_start(out=out, in_=in_), dsem, 16))
        tok = (dsem, dsem.count, "dma")
        self._record(tok, reads, writes)
        return tok

    def wait_tok(self, tok):
        self.deps(extra=(tok,))

    def replay(self, e):
        for st in self.prog:
            if st[0] == "wait":
                e.wait_ge(st[1].h, st[2])
            elif st[0] == "ins0":
                st[1](e)
            else:
                st[1](e).then_inc(st[2].h, st[3])


def _vec_layout():
    off = {}
    n = 0
    for name, w in [("c", 16), ("cctx", 16), ("norm_g", 16), ("b_sh", 16), ("b_sc", 16), ("b_in", 112),
                    ("conv_b", 16), ("ln_g", 16), ("ln_b", 16), ("conv_w", 16 * 31), ("rc_w", 2 * 16 * 4),
                    ("rc_b", 32), ("b_r", 32), ("b_i", 32), ("lam", 32), ("mL", 1), ("mR", 1), ("fm", 8), ("bm", 8),
                    ("one", 1), ("zero", 1)]:
        off[name] = n
        n += w
    return off, n


VOFF, NV = _vec_layout()


def _fm(v):
    v = np.asarray(v, np.float32).reshape(-1, 128)
    return np.ascontiguousarray(v.T)


def _blocks(w, cols_list):
    out = np.empty((len(cols_list), 128, 16 * 256), np.float32)
    for i, cols in enumerate(cols_list):
        blk = w[:, cols]
        out[i] = blk.reshape(16, 128, 256).transpose(1, 0, 2).reshape(128, 16 * 256)
    return out


def build_program():
    nc = bass.Bass("TRN2", target_bir_lowering=False)

    def din(name, shape):
        return nc.dram_tensor(name, shape, F32, kind="ExternalInput").ap()

    xp = din("xp", [TOK + 6, D])
    ctx_d = din("ctx", [CTX, D])
    vec_d = din("vec", [128, NV])
    idn_d = din("idn", [128, 128])
    fgb_d = din("fgb", [128, D])
    bgtb_d = din("bgtb", [128, D])
    wada_d = din("wada", [24, 128, 4096])
    win_d = din("win", [56, 128, 4096])
    wout_d = din("wout", [24, 128, 4096])
    gw_d = din("gw", [16, 128, 512])
    out_d = nc.dram_tensor("out", [TOK, D], F32, kind="ExternalOutput").ap()
    exi = nc.dram_tensor("exi", [128, 64], F32)
    exo = nc.dram_tensor("exo", [NCORES * 128, 64], F32)
    abd = nc.dram_tensor("abd", [NT * 16 * 4 * 128, T], F32).ap()

    with ExitStack() as es:
        ARENA_B = 212000
        arena = es.enter_context(nc.sbuf_tensor("arena", [128, ARENA_B // 2], BF16))
        cur = [0]

        def alloc(shape, dt):
            esz = 2 if dt == BF16 else 4
            n = int(np.prod(shape[1:]))
            nbytes = (n * esz + 31) // 32 * 32
            o = cur[0]
            cur[0] += nbytes
            assert cur[0] <= ARENA_B, f"arena overflow {cur[0]}"
            v = arena[:, o // 2:(o + n * esz) // 2]
            if dt == F32:
                v = v.bitcast(F32)
            if len(shape) == 3:
                v = v.rearrange("p (a b) -> p a b", a=shape[1])
            return v

        def sem(name):
            return Sem(es.enter_context(nc.semaphore(name)), name)

        E = {n: Eng(n, sem("s_" + n)) for n in ["sync", "act", "pe", "dve", "pool"]}
        SY, AC, PE, DV, PL = E["sync"], E["act"], E["pe"], E["dve"], E["pool"]

        vec = alloc([128, NV], F32); Bvec = Buf("vec")
        dv = alloc([128, 512], F32); Bdv = Buf("dv")
        DVO = {}
        _n = [0]

        def dvslot(name, w):
            DVO[name] = _n[0]
            _n[0] += w
        for nm, w in [("sh", 16), ("gsc", 16), ("shc", 16), ("gscc", 16), ("nb_r", 32), ("nb_i", 32), ("nb_zr", 16),
                      ("Lc", 32), ("L2", 32), ("ada", 64), ("silc", 32), ("tmp", 64)]:
            dvslot(nm, w)
        tot = alloc([128, 2 * 5 * 2 * 16], F32); Btot = Buf("tot")
        hin = alloc([128, 2 * 4 * 16], F32); Bhin = Buf("hin")
        exs = alloc([128, 8, 64], F32); Bexs = Buf("exs")
        exsnd = alloc([128, 64], F32); Bexsnd = Buf("exsnd")
        sml = alloc([128, 64], F32); Bsml = Buf("sml")
        identf = alloc([128, 128], F32); identb = alloc([128, 128], BF16); Bid = Buf("ident")
        onesf = alloc([128, 128], F32); Bones = Buf("ones")
        gtb = alloc([128, D], F32); Bgtb = Buf("gtb")
        fgb = alloc([128, D], F32); Bfgb = Buf("fgb")
        NSLOT = 3
        wsl = [alloc([128, 16, 256], BF16) for _ in range(NSLOT)]; Bws = [Buf(f"ws{i}") for i in range(NSLOT)]
        gws = [alloc([128, 4, 128], BF16) for _ in range(2)]; Bgw = [Buf(f"gw{i}") for i in range(2)]
        hnT = alloc([128, 16, T + 6], BF16); BhnT = Buf("hnT")
        v2 = alloc([128, 16, T], F32); Bv2 = [Buf(f"v2_{c}") for c in range(16)]
        vT = alloc([128, 16, T], BF16); BvT = [Buf(f"vT{c}") for c in range(16)]
        hT = alloc([128, 16, T], BF16); BhT = [Buf(f"hT{c}") for c in range(16)]
        yT = alloc([128, 16, T], BF16); ByT = [Buf(f"yT{c}") for c in range(16)]
        dg31 = alloc([128, 31, 128], BF16); Bdg31 = Buf("dg31")
        dg4 = [alloc([128, 8, 128], BF16) for _ in range(2)]; Bdg4 = [Buf(f"dg4{i}") for i in range(2)]
        vpad = [alloc([128, 8, 94], BF16) for _ in range(2)]; Bvpad = [Buf(f"vpad{i}") for i in range(2)]
        xrpad = [alloc([128, T + 6], BF16) for _ in range(2)]; Bxrp = [Buf(f"xrp{i}") for i in range(2)]
        s5t = [alloc([128, 256], F32) for _ in range(2)]; Bs5t = [Buf(f"s5t{i}") for i in range(2)]
        NTMP = 18
        tmpb = alloc([128, NTMP, T], F32)
        Btmp = [Buf(f"tmp{i}") for i in range(NTMP)]
        TM = [tmpb[:, i, :] for i in range(NTMP)]
        xt = tmpb[:, 0:4, :].rearrange("p a b -> p (a b)")
        Bxt = Btmp[0:4]
        xs = [tmpb[:, 4:6, :].rearrange("p a b -> p (a b)").bitcast(BF16)[:, 0:D],
              tmpb[:, 6:8, :].rearrange("p a b -> p (a b)").bitcast(BF16)[:, 0:D]]
        Bxs = [Btmp[4:6], Btmp[6:8]]

        pbank = [es.enter_context(nc.psum_tensor(f"pb{i}", [128, 512], F32))[:, :] for i in range(8)]
        BP = [Buf(f"P{i}") for i in range(8)]

        dsem_cnt = [0]

        def dsem():
            dsem_cnt[0] += 1
            return sem(f"d{dsem_cnt[0]}")

        d_misc = [dsem() for _ in range(4)]
        d_x5 = [dsem() for _ in range(4)]
        d_o = [dsem() for _ in range(4)]
        d_ws = [dsem() for _ in range(NSLOT)]
        d_gw = [dsem() for _ in range(2)]
        d_x = dsem()
        d_ex = dsem()
        d_ab = [[dsem() for _ in range(4)] for _ in range(2)]
        Babd = [[Buf(f"abd{t}_{c}") for c in range(16)] for t in range(NT)]
        cc_sem = sem("cc")

        def V(name, c=0, w=1):
            o = VOFF[name] + c
            return vec[:, o:o + w]

        def DVv(name, c=0, w=1):
            o = DVO[name] + c
            return dv[:, o:o + w]

        worder = []
        for b in range(24):
            worder.append(("ada", wada_d[b], None))
        for tl in range(5):
            for c2 in range(8):
                worder.append(("xr", win_d[24 + c2], None))
        for tl in range(4):
            for c in range(16):
                worder.append(("ag", win_d[c], None))
            for c2 in range(8):
                worder.append(("zc", win_d[16 + c2], None))
            for c2 in range(8):
                worder.append(("zr", win_d[32 + c2], None))
            for j in range(16):
                worder.append(("gg", win_d[40 + j], None))
                worder.append(("cr", wout_d[j], None))
            for nb in range(8):
                worder.append(("wo", wout_d[16 + nb], None))
        wstate = {"issued": 0, "used": 0}

        def w_issue_upto(n):
            while wstate["issued"] < min(n, len(worder)):
                i = wstate["issued"]
                kind, src, half = worder[i]
                s = i % NSLOT
                srcv = src.rearrange("p (k c) -> p k c", k=16)
                if half is not None:
                    PL.dma(d_ws[s], wsl[s][:, :, 0:half], srcv[:, :, 0:half], writes=[Bws[s]])
                else:
                    PL.dma(d_ws[s], wsl[s], srcv, writes=[Bws[s]])
                wstate["issued"] += 1

        def wget(kind):
            i = wstate["used"]
            assert worder[i][0] == kind, (worder[i][0], kind, i)
            w_issue_upto(i + NSLOT)
            wstate["used"] += 1
            s = i % NSLOT
            return wsl[s], Bws[s]

        gstate = {"n": 0}

        def gw_load(c):
            s = gstate["n"] % 2
            gstate["n"] += 1
            PL.dma(d_gw[s], gws[s], gw_d[c].rearrange("p (m j) -> p m j", m=4), writes=[Bgw[s]])
            return gws[s], Bgw[s]

        def act(out, in_, func, reads, writes, bias=None, scale=None, accum=None):
            kw = {}
            if bias is not None:
                kw["bias"] = bias
            if scale is not None:
                kw["scale"] = scale
            if accum is not None:
                kw["accum_out"] = accum
            return AC.op(lambda e: e.activation(out=out, in_=in_, func=func, **kw), reads=reads, writes=writes)

        def tt(eng, out, in0, in1, op, reads, writes):
            return eng.op(lambda e: e.tensor_tensor(out=out, in0=in0, in1=in1, op=op), reads=reads, writes=writes)

        def ts(eng, out, in0, s1, s2, op0, op1, reads, writes):
            return eng.op(lambda e: e.tensor_scalar(out=out, in0=in0, scalar1=s1, scalar2=s2, op0=op0, op1=op1),
                          reads=reads, writes=writes)

        def stt(out, in0, scalar, in1, op0, op1, reads, writes):
            return DV.op(lambda e: e.scalar_tensor_tensor(out=out, in0=in0, scalar=scalar, in1=in1, op0=op0, op1=op1),
                         reads=reads, writes=writes)

        def mm(out, lhsT, rhs, start, stop, reads, writes):
            return PE.op(lambda e: e.matmul(out, lhsT=lhsT, rhs=rhs, start=start, stop=stop), reads=reads, writes=writes,
                         inc=bool(stop))

        SY.dma(d_misc[0], vec, vec_d, writes=[Bvec])
        SY.dma(d_misc[1], identf, idn_d, writes=[Bid])
        SY.dma(d_misc[2], fgb, fgb_d, writes=[Bfgb])
        SY.dma(d_misc[3], gtb, bgtb_d, writes=[Bgtb])
        w_issue_upto(NSLOT)
        DV.op(lambda e: e.tensor_copy(out=identb, in_=identf), reads=[Bid], writes=[Bid])
        PL.op(lambda e: e.memset(onesf, 1.0), writes=[Bones])
        for i in range(2):
            PL.op(lambda e, i=i: e.memset(vpad[i], 0.0), writes=[Bvpad[i]])
        ts(DV, DVv("nb_r", 0, 32), V("b_r", 0, 32), -1.0, None, ALU.mult, ALU.bypass, [Bvec], [Bdv])
        ts(DV, DVv("nb_i", 0, 32), V("b_i", 0, 32), -1.0, None, ALU.mult, ALU.bypass, [Bvec], [Bdv])
        ts(DV, DVv("nb_zr", 0, 16), V("b_in", 64, 16), -1.0, None, ALU.mult, ALU.bypass, [Bvec], [Bdv])
        act(DVv("tmp", 0, 32), V("lam", 0, 32), AF.Exp, [Bvec], [Bdv], scale=-1.0)
        act(DVv("tmp", 32, 32), DVv("tmp", 0, 32), AF.Ln, [Bdv], [Bdv], bias=1.0)
        ts(DV, DVv("Lc", 0, 32), DVv("tmp", 32, 32), -8.0, None, ALU.mult, ALU.bypass, [Bdv], [Bdv])
        ts(DV, DVv("L2", 0, 32), DVv("tmp", 32, 32), -16.0, None, ALU.mult, ALU.bypass, [Bdv], [Bdv])
        act(DVv("tmp", 0, 32), V("c", 0, 32), AF.Exp, [Bvec], [Bdv], scale=-1.0)
        ts(DV, DVv("tmp", 0, 32), DVv("tmp", 0, 32), 1.0, None, ALU.add, ALU.bypass, [Bdv], [Bdv])
        DV.op(lambda e: e.reciprocal(out=DVv("tmp", 0, 32), in_=DVv("tmp", 0, 32)), reads=[Bdv], writes=[Bdv])
        tt(DV, DVv("silc", 0, 32), DVv("tmp", 0, 32), V("c", 0, 32), ALU.mult, [Bdv, Bvec], [Bdv])
        scb = TM[8].bitcast(BF16)[:, 0:32].rearrange("p (k n) -> p k n", n=2)
        scbb = tmpb[:, 9:11, :].rearrange("p a b -> p (a b)").bitcast(BF16).rearrange("p (k n) -> p k n", n=128)
        DV.op(lambda e: e.tensor_copy(out=scb[:, :, 0], in_=DVv("silc", 0, 16)), reads=[Bdv], writes=[Btmp[8]])
        DV.op(lambda e: e.tensor_copy(out=scb[:, :, 1], in_=DVv("silc", 16, 16)), reads=[Bdv, Btmp[8]], writes=[Btmp[8]])
        DV.op(lambda e: e.tensor_copy(out=scbb, in_=DVv("silc", 0, 16).unsqueeze(2).broadcast_to([128, 16, 128])),
              reads=[Bdv], writes=[Btmp[9], Btmp[10]])
        pada = pbank[0][:, 0:64].rearrange("p (b n) -> p b n", n=2)
        for b in range(16):
            ws, bw = wget("ada")
            for j in range(2):
                cb = 2 * b + j
                for k in range(16):
                    mm(pada[:, cb, :], ws[:, k, j * 128:(j + 1) * 128], scb[:, k, :], k == 0, k == 15,
                       [bw, Btmp[8]], [BP[0]])
        for b in range(8):
            ws, bw = wget("ada")
            bank = 4 + b // 2
            po = pbank[bank][:, (b % 2) * 256:(b % 2) * 256 + 256]
            for k in range(16):
                mm(po, scbb[:, k, :], ws[:, k, :], k == 0, k == 15, [bw, Btmp[9], Btmp[10]], [BP[bank]])
        for q in range(4):
            tt(DV, gtb[:, q * 512:(q + 1) * 512], pbank[4 + q], gtb[:, q * 512:(q + 1) * 512], ALU.add,
               [BP[4 + q], Bgtb], [Bgtb])
        adaT = DVv("ada", 0, 64).rearrange("p (b n) -> p b n", n=2)
        DV.op(lambda e: e.tensor_copy(out=adaT, in_=pada), reads=[BP[0]], writes=[Bdv])
        for n_, (shn, gscn) in enumerate([("sh", "gsc"), ("shc", "gscc")]):
            tt(DV, DVv(shn, 0, 16), adaT[:, 0:16, n_], V("b_sh", 0, 16), ALU.add, [Bdv, Bvec], [Bdv])
            tt(DV, DVv("tmp", 0, 16), adaT[:, 16:32, n_], V("b_sc", 0, 16), ALU.add, [Bdv, Bvec], [Bdv])
            stt(DVv(gscn, 0, 16), DVv("tmp", 0, 16), 1.0, V("norm_g", 0, 16), ALU.add, ALU.mult, [Bdv, Bvec], [Bdv])

        def make_hn(tile):
            ntok = CTX if tile < 0 else T
            shn, gscn = ("shc", "gscc") if tile < 0 else ("sh", "gsc")
            nblk = ntok // 128
            hview = pbank[2].bitcast(BF16)[:, 0:96].rearrange("p (c t) -> p c t", t=6)
            mview = [pbank[4 + q].bitcast(BF16).rearrange("p (c t) -> p c t", t=256) for q in range(4)]
            blocks = list(range(nblk)) + ["h"]
            for bi, b in enumerate(blocks):
                rows = 6 if b == "h" else 128
                if b == "h":
                    if tile < 0:
                        SY.dma(d_x, xt[0:6, :], ctx_d[0:6, :], writes=Bxt)
                    else:
                        SY.dma(d_x, xt[0:3, :], xp[tile * T:tile * T + 3, :], writes=Bxt)
                        SY.dma(d_x, xt[3:6, :], xp[tile * T + T + 3:tile * T + T + 6, :], writes=Bxt)
                else:
                    src = ctx_d[b * 128:(b + 1) * 128, :] if tile < 0 else xp[tile * T + 3 + b * 128:tile * T + 3 + (b + 1) * 128, :]
                    SY.dma(d_x, xt, src, writes=Bxt)
                xsb = xs[bi % 2]
                bx = Bxs[bi % 2]
                ss = sml[0:rows, bi:bi + 1]
                act(xsb[0:rows, :], xt[0:rows, :], AF.Square, Bxt, bx + [Bsml], accum=ss)
                act(sml[0:rows, 8 + bi:9 + bi], ss, AF.Ln, [Bsml], [Bsml], scale=1.0 / D, bias=EPS)
                act(sml[0:rows, 16 + bi:17 + bi], sml[0:rows, 8 + bi:9 + bi], AF.Exp, [Bsml], [Bsml], scale=-0.5)
                ts(DV, xsb[0:rows, :], xt[0:rows, :], sml[0:rows, 16 + bi:17 + bi], None, ALU.mult, ALU.bypass,
                   Bxt + [Bsml], bx)
                for c in range(16):
                    if b == "h":
                        PE.op(lambda e, c=c, xsb=xsb: e.transpose(out=hview[:, c, :], in_=xsb[0:6, c * 128:(c + 1) * 128],
                                                                   identity=identb[0:6, 0:6]),
                              reads=bx + [Bid], writes=[BP[2]], inc=(c == 15))
                    else:
                        half = b // 2
                        col = (b % 2) * 128
                        PE.op(lambda e, c=c, xsb=xsb, col=col: e.transpose(out=mview[c // 4][:, c % 4, col:col + 128],
                                                                           in_=xsb[:, c * 128:(c + 1) * 128], identity=identb),
                              reads=bx + [Bid], writes=[BP[4 + c // 4]], inc=(c % 4 == 3))
                if b != "h" and b % 2 == 1:
                    half = b // 2
                    for c in range(16):
                        act(hnT[:, c, half * 256:half * 256 + 256], mview[c // 4][:, c % 4, :], AF.Identity,
                            [BP[4 + c // 4], Bdv], [BhnT], bias=DVv(shn, c), scale=DVv(gscn, c))
                if b == "h":
                    for c in range(16):
                        ts(DV, hnT[:, c, ntok:ntok + 3], hview[:, c, 0:3], DVv(gscn, c), DVv(shn, c), ALU.mult, ALU.add,
                           [BP[2], Bdv], [BhnT])
                        ts(DV, hnT[:, c, ntok + 3:ntok + 6], hview[:, c, 3:6], DVv(gscn, c), DVv(shn, c), ALU.mult, ALU.add,
                           [BP[2], Bdv], [BhnT])

        RSET = [[(TM[8 + i], Btmp[8 + i]) for i in range(10)],
                [(v2[:, i, :], Bv2[i]) for i in range(10)]]

        def rnn_front(tile, c, ws, bw, j):
            ntok = CTX if tile < 0 else T
            ti = tile + 1
            par = c % 2
            if tile < 0:
                mLv, mRv = V("zero"), V("zero")
            else:
                mLv = V("mL") if tile == 0 else V("one")
                mRv = V("mR") if tile == NT - 1 else V("one")
            b_xr = V("b_in", 48 + c)
            wcol = slice(j * 128, (j + 1) * 128)
            pX = pbank[par]; bpX = BP[par]
            pH = pbank[2 + par][:, 0:6]; bpH = BP[2 + par]
            for k in range(16):
                mm(pX[:, 0:ntok], ws[:, k, wcol], hnT[:, k, 0:ntok], k == 0, k == 15, [bw, BhnT], [bpX])
            for k in range(16):
                mm(pH, ws[:, k, wcol], hnT[:, k, ntok:ntok + 6], k == 0, k == 15, [bw, BhnT], [bpH])
            xrp = xrpad[par]; bxrp = Bxrp[par]
            act(xrp[:, 3:3 + ntok], pX[:, 0:ntok], AF.Identity, [bpX, Bvec], [bxrp], bias=b_xr)
            ts(DV, xrp[:, 0:3], pH[:, 0:3], b_xr, mLv, ALU.add, ALU.mult, [bpH, Bvec], [bxrp])
            ts(DV, xrp[:, ntok + 3:ntok + 6], pH[:, 3:6], b_xr, mRv, ALU.add, ALU.mult, [bpH, Bvec], [bxrp])
            dg = dg4[par]; bdg = Bdg4[par]
            wv = V("rc_w", 0, 128).rearrange("p (d c k) -> p d c k", d=2, c=16)
            for d in range(2):
                PL.op(lambda e, d=d: e.tensor_tensor(out=dg[:, d * 4:(d + 1) * 4, :],
                                                     in0=identf.unsqueeze(1).broadcast_to([128, 4, 128]),
                                                     in1=wv[:, d, c, :].unsqueeze(2).broadcast_to([128, 4, 128]), op=ALU.mult),
                      reads=[Bid, Bvec], writes=[bdg])
            gwt, bgw = gw_load(c)
            for d in range(2):
                bs = RSET[par][d * 5:(d + 1) * 5]
                pXc = pbank[4 + d]; bpXc = BP[4 + d]
                for k in range(4):
                    off = k if d == 0 else 3 + k
                    mm(pXc[:, 0:ntok], dg[:, d * 4 + k, :], xrp[:, off:off + ntok], k == 0, k == 3, [bdg, bxrp], [bpXc])
                xc = bs[0][0][:, 0:ntok]; bxc = bs[0][1]
                act(xc, pXc[:, 0:ntok], AF.Identity, [bpXc, Bvec], [bxc], bias=V("rc_b", d * 16 + c))
                xcb = bs[1][0].bitcast(BF16)[:, 0:ntok]; bxcb = bs[1][1]
                DV.op(lambda e, xcb=xcb, xc=xc: e.tensor_copy(out=xcb, in_=xc), reads=[bxc], writes=[bxcb])
                pr_, bpr = pbank[6], BP[6]
                pi_, bpi = pbank[7], BP[7]
                mm(pr_[:, 0:ntok], gwt[:, d, :], xcb, True, True, [bgw, bxcb], [bpr])
                mm(pi_[:, 0:ntok], gwt[:, 2 + d, :], xcb, True, True, [bgw, bxcb], [bpi])
                r = bs[2][0][:, 0:ntok]; br = bs[2][1]
                ii = bs[3][0][:, 0:ntok]; bi_ = bs[3][1]
                aa = bs[4][0][:, 0:ntok]; ba = bs[4][1]
                act(r, pr_[:, 0:ntok], AF.Exp, [bpr, Bdv], [br], bias=DVv("nb_r", d * 16 + c), scale=-1.0)
                act(ii, pi_[:, 0:ntok], AF.Exp, [bpi, Bdv], [bi_], bias=DVv("nb_i", d * 16 + c), scale=-1.0)
                act(r, r, AF.Ln, [br], [br], bias=1.0)
                act(r, r, AF.Exp, [br], [br], scale=-1.0)
                DV.op(lambda e, r=r, d=d: e.reduce_sum(out=sml[:, 32 + 2 * par + d:33 + 2 * par + d], in_=r, axis=AX.X),
                      reads=[br], writes=[Bsml])
                act(aa, r, AF.Exp, [br, Bdv], [ba], scale=DVv("Lc", d * 16 + c))
                act(r, r, AF.Exp, [br, Bdv], [br], scale=DVv("L2", d * 16 + c))
                act(r, r, AF.Ln, [br], [br], scale=-1.0, bias=1.0)
                act(ii, ii, AF.Ln, [bi_], [bi_], bias=1.0)
                stt(ii, r, 0.5, ii, ALU.mult, ALU.subtract, [br, bi_], [bi_])
                act(ii, ii, AF.Exp, [bi_], [bi_])
                tt(DV, ii, ii, xc, ALU.mult, [bi_, bxc], [bi_])
                h = xc
                if d == 0:
                    DV.op(lambda e, h=h, aa=aa, ii=ii: e.tensor_tensor_scan(out=h, data0=aa, data1=ii, initial=0.0,
                                                                            op0=ALU.mult, op1=ALU.add),
                          reads=[ba, bi_], writes=[bxc])
                else:
                    DV.op(lambda e, h=h, aa=aa, ii=ii: e.tensor_tensor_scan(out=h[:, ::-1], data0=aa[:, ::-1],
                                                                            data1=ii[:, ::-1], initial=0.0,
                                                                            op0=ALU.mult, op1=ALU.add),
                          reads=[ba, bi_], writes=[bxc])
                base = ((d * 5 + ti) * 2) * 16
                act(tot[:, base + c:base + c + 1], sml[:, 32 + 2 * par + d:33 + 2 * par + d], AF.Exp, [Bsml, Bdv], [Btot],
                    scale=DVv("Lc", d * 16 + c))
                last = h[:, ntok - 1:ntok] if d == 0 else h[:, 0:1]
                DV.op(lambda e, base=base, last=last: e.tensor_copy(out=tot[:, base + 16 + c:base + 16 + c + 1], in_=last),
                      reads=[bxc], writes=[Btot])
                if tile >= 0:
                    row = ((tile * 16 + c) * 4 + 2 * d) * 128
                    SY.dma(d_ab[par][2 * d], abd[row:row + 128, :], aa, reads=[ba], writes=[Babd[tile][c]])
                    SY.dma(d_ab[par][2 * d + 1], abd[row + 128:row + 256, :], ii, reads=[bi_], writes=[Babd[tile][c]])

        def rnn_load(tile, c):
            par = c % 2
            for q in range(4):
                row = ((tile * 16 + c) * 4 + q) * 128
                SY.dma(d_ab[par][q], v2[:, par * 4 + q, :], abd[row:row + 128, :], reads=[Babd[tile][c]],
                       writes=[Bv2[par * 4 + q]])

        def rnn_back(tile, c, ws, bw, j):
            par = c % 2
            pZ = pbank[par]; bpZ = BP[par]
            for k in range(16):
                mm(pZ, ws[:, k, j * 128:(j + 1) * 128], hnT[:, k, 0:T], k == 0, k == 15, [bw, BhnT], [bpZ])
            sz = TM[8 + par]; bsz = Btmp[8 + par]
            act(sz, pZ, AF.Silu, [bpZ, Bvec], [bsz], bias=V("b_in", 64 + c))
            a_f, b_f, a_b, b_b = [v2[:, par * 4 + q, :] for q in range(4)]
            Ba_f, Bb_f, Ba_b, Bb_b = [Bv2[par * 4 + q] for q in range(4)]
            hf = TM[10 + par]; bhf = Btmp[10 + par]
            hb = TM[12 + par]; bhb = Btmp[12 + par]
            inf = hin[:, (0 * 4 + tile) * 16 + c:(0 * 4 + tile) * 16 + c + 1]
            inb = hin[:, (1 * 4 + tile) * 16 + c:(1 * 4 + tile) * 16 + c + 1]
            DV.op(lambda e: e.tensor_tensor_scan(out=hf, data0=a_f, data1=b_f, initial=inf, op0=ALU.mult, op1=ALU.add),
                  reads=[Ba_f, Bb_f, Bhin], writes=[bhf])
            DV.op(lambda e: e.tensor_tensor_scan(out=hb[:, ::-1], data0=a_b[:, ::-1], data1=b_b[:, ::-1], initial=inb,
                                                 op0=ALU.mult, op1=ALU.add),
                  reads=[Ba_b, Bb_b, Bhin], writes=[bhb])
            tt(PL, hf, hf, hb, ALU.add, [bhf, bhb], [bhf])
            tt(DV, hT[:, c, :], hf, sz, ALU.mult, [bhf, bsz], [BhT[c]])

        for tile in [-1, 0, 1, 2, 3]:
            make_hn(tile)
            for c2 in range(8):
                ws, bw = wget("xr")
                for j in range(2):
                    rnn_front(tile, 2 * c2 + j, ws, bw, j)

        def TOT(d, ti, ab, w=16):
            o = ((d * 5 + ti) * 2 + ab) * 16
            return tot[:, o:o + w]

        def small(eng, fn, reads, writes):
            return eng.op(fn, reads=reads, writes=writes)

        for d in range(2):
            Aacc = exsnd[:, d * 32:d * 32 + 16]
            Bacc = exsnd[:, d * 32 + 16:d * 32 + 32]
            order = [1, 2, 3, 4] if d == 0 else [4, 3, 2, 1]
            first = order[0]
            DV.op(lambda e, Aacc=Aacc, d=d, first=first: e.tensor_copy(out=Aacc, in_=TOT(d, first, 0)), reads=[Btot], writes=[Bexsnd])
            DV.op(lambda e, Bacc=Bacc, d=d, first=first: e.tensor_copy(out=Bacc, in_=TOT(d, first, 1)), reads=[Btot, Bexsnd], writes=[Bexsnd])
            for ti in order[1:]:
                tt(DV, Bacc, Bacc, TOT(d, ti, 0), ALU.mult, [Bexsnd, Btot], [Bexsnd])
                tt(DV, Bacc, Bacc, TOT(d, ti, 1), ALU.add, [Bexsnd, Btot], [Bexsnd])
                tt(DV, Aacc, Aacc, TOT(d, ti, 0), ALU.mult, [Bexsnd, Btot], [Bexsnd])
        Bexd = Buf("exdram")
        PL.dma(d_ex, exi.ap(), exsnd, reads=[Bexsnd], writes=[Bexd])
        PL.deps(reads=[Bexd])
        cc_sem.count += 1
        PL.prog.append(("ins", lambda e: e.collective_compute("AllGather", ALU.bypass, replica_groups=[list(range(NCORES))],
                                                             ins=[exi.ap().opt()], outs=[exo.ap().opt()]), cc_sem, 1))
        Bexd.last_write = (cc_sem, cc_sem.count, "cc")
        Bexd.reads = {}
        PL.dma(d_ex, exs, exo.ap().rearrange("(r p) f -> p r f", p=128), reads=[Bexd], writes=[Bexs])
        hcur = DVv("tmp", 0, 16)
        tmpv = DVv("tmp", 16, 16)
        for d in range(2):
            DV.op(lambda e, d=d: e.tensor_copy(out=hcur, in_=TOT(d, 0, 1)), reads=[Btot, Bdv], writes=[Bdv])
            cores = range(NCORES) if d == 0 else range(NCORES - 1, -1, -1)
            mname = "fm" if d == 0 else "bm"
            for j in cores:
                Aj = exs[:, j, d * 32:d * 32 + 16]
                Bj = exs[:, j, d * 32 + 16:d * 32 + 32]
                tt(DV, tmpv, hcur, Aj, ALU.mult, [Bdv, Bexs], [Bdv])
                tt(DV, tmpv, tmpv, Bj, ALU.add, [Bdv, Bexs], [Bdv])
                tt(DV, tmpv, tmpv, hcur, ALU.subtract, [Bdv], [Bdv])
                stt(hcur, tmpv, V(mname, j), hcur, ALU.mult, ALU.add, [Bdv, Bvec], [Bdv])
            tiles = [0, 1, 2, 3] if d == 0 else [3, 2, 1, 0]
            for n_, tl in enumerate(tiles):
                dst = hin[:, (d * 4 + tl) * 16:(d * 4 + tl) * 16 + 16]
                if n_ == 0:
                    DV.op(lambda e, dst=dst: e.tensor_copy(out=dst, in_=hcur), reads=[Bdv, Bhin], writes=[Bhin])
                else:
                    prev = tiles[n_ - 1]
                    src = hin[:, (d * 4 + prev) * 16:(d * 4 + prev) * 16 + 16]
                    tt(DV, dst, src, TOT(d, prev + 1, 0), ALU.mult, [Bhin, Btot], [Bhin])
                    tt(DV, dst, dst, TOT(d, prev + 1, 1), ALU.add, [Bhin, Btot], [Bhin])

        cw = V("conv_w", 0, 496).rearrange("p (c k) -> p c k", k=31)
        for tile in range(NT):
            make_hn(tile)
            s1 = TM[8]; s2 = TM[9]; bs1 = Btmp[8]; bs2 = Btmp[9]
            for c in range(16):
                ws, bw = wget("ag")
                par = c % 2
                pa, bpa = pbank[2 * par], BP[2 * par]
                pg, bpg = pbank[2 * par + 1], BP[2 * par + 1]
                for k in range(16):
                    mm(pa, ws[:, k, 0:128], hnT[:, k, 0:T], k == 0, k == 15, [bw, BhnT], [bpa])
                for k in range(16):
                    mm(pg, ws[:, k, 128:256], hnT[:, k, 0:T], k == 0, k == 15, [bw, BhnT], [bpg])
                sg = TM[10 + par]; bsg = Btmp[10 + par]
                act(sg, pg, AF.Sigmoid, [bpg, Bvec], [bsg], bias=V("b_in", 16 + c))
                vp = vpad[par]; bvp = Bvpad[par]
                stt(vp[:, :, 15:79], pa.rearrange("p (r t) -> p r t", t=64), V("b_in", c),
                    sg.rearrange("p (r t) -> p r t", t=64), ALU.add, ALU.mult, [bpa, bsg, Bvec], [bvp])
                PL.op(lambda e, c=c: e.tensor_tensor(out=dg31, in0=identf.unsqueeze(1).broadcast_to([128, 31, 128]),
                                                     in1=cw[:, c, :].unsqueeze(2).broadcast_to([128, 31, 128]), op=ALU.mult),
                      reads=[Bid, Bvec], writes=[Bdg31])
                pc, bpc = pbank[4 + par], BP[4 + par]
                for k in range(31):
                    mm(pc.rearrange("p (r t) -> p r t", t=64), dg31[:, k, :], vp[:, :, k:k + 64], k == 0, k == 30,
                       [Bdg31, bvp], [bpc])
                act(v2[:, c, :], pc, AF.Identity, [bpc, Bvec], [Bv2[c]], bias=V("conv_b", c))
                sq = TM[12 + par]; bsq = Btmp[12 + par]
                act(sq, pc, AF.Square, [bpc, Bvec], [bsq], bias=V("conv_b", c))
                if c == 0:
                    PL.op(lambda e: e.tensor_copy(out=s1, in_=v2[:, 0, :]), reads=[Bv2[0]], writes=[bs1])
                    PL.op(lambda e, sq=sq: e.tensor_copy(out=s2, in_=sq), reads=[bsq], writes=[bs2])
                else:
                    tt(PL, s1, s1, v2[:, c, :], ALU.add, [bs1, Bv2[c]], [bs1])
                    tt(PL, s2, s2, sq, ALU.add, [bs2, bsq], [bs2])
            mm(pbank[6], onesf, s1, True, True, [Bones, bs1], [BP[6]])
            mm(pbank[7], onesf, s2, True, True, [Bones, bs2], [BP[7]])
            mean = TM[14]; bmean = Btmp[14]
            rstd = TM[15]; brstd = Btmp[15]
            ts(DV, mean, pbank[6], 1.0 / D, None, ALU.mult, ALU.bypass, [BP[6]], [bmean])
            ts(DV, rstd, pbank[7], 1.0 / D, None, ALU.mult, ALU.bypass, [BP[7]], [brstd])
            msq = TM[16]; bmsq = Btmp[16]
            tt(DV, msq, mean, mean, ALU.mult, [bmean], [bmsq])
            tt(DV, rstd, rstd, msq, ALU.subtract, [brstd, bmsq], [brstd])
            act(rstd, rstd, AF.Ln, [brstd], [brstd], bias=EPS)
            act(rstd, rstd, AF.Exp, [brstd], [brstd], scale=-0.5)
            for c2 in range(8):
                ws, bw = wget("zc")
                for j in range(2):
                    c = 2 * c2 + j
                    par = c % 2
                    pz, bpz = pbank[par], BP[par]
                    for k in range(16):
                        mm(pz, ws[:, k, j * 128:(j + 1) * 128], hnT[:, k, 0:T], k == 0, k == 15, [bw, BhnT], [bpz])
                    sz = TM[10 + par]; bsz = Btmp[10 + par]
                    act(sz, pz, AF.Silu, [bpz, Bvec], [bsz], bias=V("b_in", 32 + c))
                    t1 = TM[12 + par]; bt1 = Btmp[12 + par]
                    tt(PL, t1, v2[:, c, :], mean, ALU.subtract, [Bv2[c], bmean], [bt1])
                    tt(DV, t1, t1, rstd, ALU.mult, [bt1, brstd], [bt1])
                    act(t1, t1, AF.Silu, [bt1, Bvec], [bt1], bias=V("ln_b", c), scale=V("ln_g", c))
                    tt(DV, vT[:, c, :], t1, sz, ALU.mult, [bt1, bsz], [BvT[c]])
            rnn_load(tile, 0)
            for c2 in range(8):
                ws, bw = wget("zr")
                for j in range(2):
                    c = 2 * c2 + j
                    if c + 1 < 16:
                        rnn_load(tile, c + 1)
                    rnn_back(tile, c, ws, bw, j)
            for j in range(16):
                ws, bw = wget("gg")
                par = j % 2
                pgc, bpgc = pbank[4 * par], BP[4 * par]
                pgr, bpgr = pbank[4 * par + 1], BP[4 * par + 1]
                pyc, bpyc = pbank[4 * par + 2], BP[4 * par + 2]
                pyr, bpyr = pbank[4 * par + 3], BP[4 * par + 3]
                for k in range(16):
                    mm(pgc, ws[:, k, 0:128], hnT[:, k, 0:T], k == 0, k == 15, [bw, BhnT], [bpgc])
                for k in range(16):
                    mm(pgr, ws[:, k, 128:256], hnT[:, k, 0:T], k == 0, k == 15, [bw, BhnT], [bpgr])
                ws2, bw2 = wget("cr")
                for k in range(16):
                    mm(pyc, ws2[:, k, 0:128], vT[:, k, :], k == 0, k == 15, [bw2, BvT[k]], [bpyc])
                for k in range(16):
                    mm(pyr, ws2[:, k, 128:256], hT[:, k, :], k == 0, k == 15, [bw2, BhT[k]], [bpyr])
                sgc = TM[10 + par]; bsgc = Btmp[10 + par]
                sgr = TM[12 + par]; bsgr = Btmp[12 + par]
                act(sgc, pgc, AF.Sigmoid, [bpgc, Bvec], [bsgc], bias=V("b_in", 80 + j))
                act(sgr, pgr, AF.Sigmoid, [bpgr, Bvec], [bsgr], bias=V("b_in", 96 + j))
                tt(DV, sgc, pyc, sgc, ALU.mult, [bpyc, bsgc], [bsgc])
                tt(DV, sgr, pyr, sgr, ALU.mult, [bpyr, bsgr], [bsgr])
                tt(PL, yT[:, j, :], sgc, sgr, ALU.add, [bsgc, bsgr], [ByT[j]])
            xn = v2.rearrange("p a b -> p (a b)").rearrange("p (t n) -> p t n", t=4)
            for tb in range(4):
                r0 = tile * T + 3 + tb * 128
                SY.dma(d_x5[tb], xn[:, tb, :], xp[r0:r0 + 128, :], writes=Bv2[4 * tb:4 * tb + 4])
            for nb in range(8):
                ws, bw = wget("wo")
                for tb in range(4):
                    par = (nb * 4 + tb) % 2
                    po, bpo = pbank[par][:, 0:256], BP[par]
                    for k in range(16):
                        mm(po, yT[:, k, tb * 128:(tb + 1) * 128], ws[:, k, :], k == 0, k == 15, [ByT[k], bw], [bpo])
                    tmp5 = s5t[par]; bt5 = Bs5t[par]
                    tt(DV, tmp5, po, gtb[:, nb * 256:(nb + 1) * 256], ALU.mult, [bpo, Bgtb], [bt5])
                    bx = Bv2[4 * tb + nb // 2]
                    tt(PL, xn[:, tb, nb * 256:(nb + 1) * 256], xn[:, tb, nb * 256:(nb + 1) * 256], tmp5, ALU.add,
                       [bx, bt5], [bx])
            for tb in range(4):
                bxs_ = Bv2[4 * tb:4 * tb + 4]
                junk = tmpb[:, 4:6, :].rearrange("p a b -> p (a b)").bitcast(BF16)[:, 0:D]
                act(junk, xn[:, tb, :], AF.Square, bxs_, Btmp[4:6] + [Bsml], accum=sml[:, 40 + tb:41 + tb])
                act(sml[:, 44 + tb:45 + tb], sml[:, 40 + tb:41 + tb], AF.Ln, [Bsml], [Bsml], scale=1.0 / D, bias=EPS)
                act(sml[:, 48 + tb:49 + tb], sml[:, 44 + tb:45 + tb], AF.Exp, [Bsml], [Bsml], scale=-0.5)
                stt(xn[:, tb, :], xn[:, tb, :], sml[:, 48 + tb:49 + tb], fgb, ALU.mult, ALU.mult, bxs_ + [Bsml, Bfgb], bxs_)
                r0 = tile * T + tb * 128
                SY.dma(d_o[tb], out_d[r0:r0 + 128, :], xn[:, tb, :], reads=bxs_)

        for tb in range(4):
            SY.wait_tok((d_o[tb], d_o[tb].count, "dma"))
        assert wstate["used"] == len(worder), (wstate, len(worder))

        with nc.Block() as block:
            @block.sync
            def _(e):
                SY.replay(e)

            @block.scalar
            def _(e):
                AC.replay(e)

            @block.tensor
            def _(e):
                PE.replay(e)

            @block.vector
            def _(e):
                DV.replay(e)

            @block.gpsimd
            def _(e):
                PL.replay(e)
    return nc


_CACHE = {}


def kernel(x, c, ctx, c_ctx, w_ada, b_ada, norm_g, w_in, b_in, conv_dw, conv_dw_b, conv_ln_g, conv_ln_b, w_conv_out,
           rnn_conv, rnn_conv_b, rnn_w_r, rnn_b_r, rnn_w_i, rnn_b_i, rnn_lam, w_rnn_out, w_o, final_g):
    f = lambda a: np.asarray(a, np.float32)
    x = f(x)[0]
    ctx = f(ctx)[0]
    w_ada, b_ada, norm_g, w_in, b_in = f(w_ada)[0], f(b_ada)[0], f(norm_g)[0], f(w_in)[0], f(b_in)[0]
    conv_dw, conv_dw_b, conv_ln_g, conv_ln_b = f(conv_dw)[0], f(conv_dw_b)[0], f(conv_ln_g)[0], f(conv_ln_b)[0]
    w_conv_out, rnn_conv, rnn_conv_b = f(w_conv_out)[0], f(rnn_conv)[0], f(rnn_conv_b)[0]
    rnn_w_r, rnn_b_r, rnn_w_i, rnn_b_i, rnn_lam = f(rnn_w_r)[0], f(rnn_b_r)[0], f(rnn_w_i)[0], f(rnn_b_i)[0], f(rnn_lam)[0]
    w_rnn_out, w_o, final_g = f(w_rnn_out)[0], f(w_o)[0], f(final_g)

    ar = np.arange
    ada_cols = [ar(b * 256, (b + 1) * 256) for b in range(24)]
    wada_b = _blocks(w_ada, ada_cols)
    G = D
    in_cols = []
    for cc in range(16):
        in_cols.append(np.concatenate([ar(cc * 128, cc * 128 + 128), G + ar(cc * 128, cc * 128 + 128)]))
    for c2 in range(8):
        in_cols.append(2 * G + ar(c2 * 256, c2 * 256 + 256))
    for c2 in range(8):
        in_cols.append(3 * G + ar(c2 * 256, c2 * 256 + 256))
    for c2 in range(8):
        in_cols.append(4 * G + ar(c2 * 256, c2 * 256 + 256))
    for j in range(16):
        in_cols.append(np.concatenate([5 * G + ar(j * 128, j * 128 + 128), 6 * G + ar(j * 128, j * 128 + 128)]))
    win_b = _blocks(w_in, in_cols)
    wcr = np.concatenate([w_conv_out, w_rnn_out], axis=1)
    out_cols = [np.concatenate([ar(j * 128, j * 128 + 128), D + ar(j * 128, j * 128 + 128)]) for j in range(16)]
    wout_b = np.concatenate([_blocks(wcr, out_cols), _blocks(w_o, [ar(nb * 256, nb * 256 + 256) for nb in range(8)])], axis=0)
    gw = np.empty((16, 128, 4, 128), np.float32)
    for h in range(16):
        gw[h, :, 0, :] = rnn_w_r[0, h]
        gw[h, :, 1, :] = rnn_w_r[1, h]
        gw[h, :, 2, :] = rnn_w_i[0, h]
        gw[h, :, 3, :] = rnn_w_i[1, h]
    gw = gw.reshape(16, 128, 512)

    def make_vec(core):
        v = np.zeros((128, NV), np.float32)

        def put(name, arr):
            arr = np.asarray(arr, np.float32)
            v[:, VOFF[name]:VOFF[name] + arr.shape[1]] = arr
        put("c", _fm(f(c)[0]))
        put("cctx", _fm(f(c_ctx)))
        put("norm_g", _fm(norm_g))
        put("b_sh", _fm(b_ada[0:D]))
        put("b_sc", _fm(b_ada[D:2 * D]))
        put("b_in", _fm(b_in))
        put("conv_b", _fm(conv_dw_b))
        put("ln_g", _fm(conv_ln_g))
        put("ln_b", _fm(conv_ln_b))
        put("conv_w", conv_dw.T.reshape(16, 128, 31).transpose(1, 0, 2).reshape(128, 496))
        put("rc_w", rnn_conv.transpose(2, 0, 1).reshape(16, 128, 2, 4).transpose(1, 2, 0, 3).reshape(128, 128))
        put("rc_b", np.concatenate([_fm(rnn_conv_b[0]), _fm(rnn_conv_b[1])], axis=1))
        put("b_r", np.concatenate([_fm(rnn_b_r[0]), _fm(rnn_b_r[1])], axis=1))
        put("b_i", np.concatenate([_fm(rnn_b_i[0]), _fm(rnn_b_i[1])], axis=1))
        put("lam", np.concatenate([_fm(rnn_lam[0]), _fm(rnn_lam[1])], axis=1))
        put("mL", np.full((128, 1), 0.0 if core == 0 else 1.0))
        put("mR", np.full((128, 1), 0.0 if core == NCORES - 1 else 1.0))
        put("fm", np.tile((np.arange(8) < core).astype(np.float32)[None, :], (128, 1)))
        put("bm", np.tile((np.arange(8) > core).astype(np.float32)[None, :], (128, 1)))
        put("one", np.ones((128, 1)))
        return v

    fgb = np.ascontiguousarray(np.broadcast_to(final_g[None, :], (128, D)))
    bgtb = np.ascontiguousarray(np.broadcast_to(b_ada[None, 2 * D:3 * D], (128, D)))
    idn = np.eye(128, dtype=np.float32)
    in_maps = []
    for k in range(NCORES):
        xpk = np.zeros((TOK + 6, D), np.float32)
        lo = k * TOK - 3
        hi = k * TOK + TOK + 3
        s0 = max(lo, 0)
        s1 = min(hi, x.shape[0])
        xpk[s0 - lo:s1 - lo] = x[s0:s1]
        in_maps.append({"xp": xpk, "ctx": ctx, "vec": make_vec(k), "idn": idn, "fgb": fgb, "bgtb": bgtb,
                        "wada": wada_b, "win": win_b, "wout": wout_b, "gw": gw})
    if "nc" not in _CACHE:
        _CACHE["nc"] = build_program()
    res = run_bass_kernel_spmd(_CACHE["nc"], in_maps, core_ids=list(range(NCORES)))
    out = np.concatenate([r["out"] for r in res.results], axis=0)
    return out[None].astype(np.float32)
```

```python
from contextlib import ExitStack

import numpy as np
import concourse.bass as bass
import concourse.mybir as mybir
from concourse.bass_utils import run_bass_kernel_spmd

F32 = mybir.dt.float32
BF16 = mybir.dt.bfloat16
AF = mybir.ActivationFunctionType
ALU = mybir.AluOpType
AX = mybir.AxisListType

NCORES = 8
D = 2048
NCH = 16
TOK = 2048
T = 512
NT = TOK // T
CTX = 256
EPS = 1e-6


class Sem:
    def __init__(self, handle, name):
        self.h = handle
        self.name = name
        self.count = 0


class Buf:
    def __init__(self, name):
        self.name = name
        self.last_write = None
        self.reads = {}


class Eng:
    def __init__(self, name, sem):
        self.name = name
        self.sem = sem
        self.waited = {}
        self.prog = []

    def _need(self, pending, tok, same_ok):
        if tok is None:
            return
        sem, val, en = tok
        if same_ok and en == self.name:
            return
        if self.waited.get(sem, 0) >= val:
            return
        pending[sem] = max(pending.get(sem, 0), val)

    def deps(self, reads=(), writes=(), extra=()):
        pending = {}
        for b in reads:
            self._need(pending, b.last_write, False)
        for b in writes:
            self._need(pending, b.last_write, True)
            for rs, (rv, ren) in b.reads.items():
                self._need(pending, (rs, rv, ren), True)
        for tok in extra:
            self._need(pending, tok, False)
        for sem, val in pending.items():
            self.prog.append(("wait", sem, val))
            self.waited[sem] = val

    def _record(self, tok, reads, writes):
        for b in reads:
            old = b.reads.get(tok[0])
            if old is None or old[0] < tok[1]:
                b.reads[tok[0]] = (tok[1], tok[2])
        for b in writes:
            b.last_write = tok
            b.reads = {}

    def op(self, fn, reads=(), writes=(), extra=(), inc=True):
        self.deps(reads, writes, extra)
        if inc:
            self.sem.count += 1
            self.prog.append(("ins", fn, self.sem, 1))
            tok = (self.sem, self.sem.count, self.name)
        else:
            self.prog.append(("ins0", fn))
            tok = (self.sem, self.sem.count + 1, self.name)
        self._record(tok, reads, writes)
        return tok

    def dma(self, dsem, out, in_, reads=(), writes=(), extra=()):
        self.deps(reads, writes, extra)
        dsem.count += 16
        self.prog.append(("ins", lambda e: e.dma_start(out=out, in_=in_), dsem, 16))
        tok = (dsem, dsem.count, "dma")
        self._record(tok, reads, writes)
        return tok

    def wait_tok(self, tok):
        self.deps(extra=(tok,))

    def replay(self, e):
        for st in self.prog:
            if st[0] == "wait":
                e.wait_ge(st[1].h, st[2])
            elif st[0] == "ins0":
                st[1](e)
            else:
                st[1](e).then_inc(st[2].h, st[3])


def _vec_layout():
    off = {}
    n = 0
    for name, w in [("c", 16), ("cctx", 16), ("norm_g", 16), ("b_sh", 16), ("b_sc", 16), ("b_in", 112),
                    ("conv_b", 16), ("ln_g", 16), ("ln_b", 16), ("conv_w", 16 * 31), ("rc_w", 2 * 16 * 4),
                    ("rc_b", 32), ("b_r", 32), ("b_i", 32), ("lam", 32), ("mL", 1), ("mR", 1), ("fm", 8), ("bm", 8),
                    ("one", 1), ("zero", 1)]:
        off[name] = n
        n += w
    return off, n


VOFF, NV = _vec_layout()


def _fm(v):
    v = np.asarray(v, np.float32).reshape(-1, 128)
    return np.ascontiguousarray(v.T)


def _blocks(w, cols_list):
    out = np.empty((len(cols_list), 128, 16 * 256), np.float32)
    for i, cols in enumerate(cols_list):
        blk = w[:, cols]
        out[i] = blk.reshape(16, 128, 256).transpose(1, 0, 2).reshape(128, 16 * 256)
    return out


_DEBUG = {}


def build_program():
    nc = bass.Bass("TRN2", target_bir_lowering=False)

    def din(name, shape):
        return nc.dram_tensor(name, shape, F32, kind="ExternalInput").ap()

    xp = din("xp", [TOK + 6, D])
    ctx_d = din("ctx", [CTX, D])
    vec_d = din("vec", [128, NV])
    idn_d = din("idn", [128, 128])
    fgb_d = din("fgb", [128, D])
    bgtb_d = din("bgtb", [128, D])
    wada_d = din("wada", [24, 128, 4096])
    win_d = din("win", [56, 128, 4096])
    wout_d = din("wout", [24, 128, 4096])
    gw_d = din("gw", [16, 128, 512])
    out_d = nc.dram_tensor("out", [TOK, D], F32, kind="ExternalOutput").ap()
    exi = nc.dram_tensor("exi", [128, 64], F32)
    exo = nc.dram_tensor("exo", [NCORES * 128, 64], F32)
    abd = nc.dram_tensor("abd", [NT * 16 * 4 * 128, T], F32).ap()
    vtd = nc.dram_tensor("vtd", [NT * 128, 16 * T], BF16).ap()

    with ExitStack() as es:
        ARENA_B = 212000
        arena = es.enter_context(nc.sbuf_tensor("arena", [128, ARENA_B // 2], BF16))
        cur = [0]

        def alloc(shape, dt):
            esz = 2 if dt == BF16 else 4
            n = int(np.prod(shape[1:]))
            nbytes = (n * esz + 31) // 32 * 32
            o = cur[0]
            cur[0] += nbytes
            assert cur[0] <= ARENA_B, f"arena overflow {cur[0]}"
            v = arena[:, o // 2:(o + n * esz) // 2]
            if dt == F32:
                v = v.bitcast(F32)
            if len(shape) == 3:
                v = v.rearrange("p (a b) -> p a b", a=shape[1])
            return v

        def sem(name):
            return Sem(es.enter_context(nc.semaphore(name)), name)

        E = {n: Eng(n, sem("s_" + n)) for n in ["sync", "act", "pe", "dve", "pool"]}
        SY, AC, PE, DV, PL = E["sync"], E["act"], E["pe"], E["dve"], E["pool"]

        vec = alloc([128, NV], F32); Bvec = Buf("vec")
        dv = alloc([128, 512], F32); Bdv = Buf("dv")
        DVO = {}
        _n = [0]

        def dvslot(name, w):
            DVO[name] = _n[0]
            _n[0] += w
        for nm, w in [("sh", 16), ("gsc", 16), ("shc", 16), ("gscc", 16), ("nb_r", 32), ("nb_i", 32), ("nb_zr", 16), ("nb_g", 16),
                      ("Lc", 32), ("L2", 32), ("ada", 64), ("silc", 32), ("tmp", 64)]:
            dvslot(nm, w)
        tot = alloc([128, 2 * 5 * 2 * 16], F32); Btot = Buf("tot")
        hin = alloc([128, 2 * 4 * 16], F32); Bhin = Buf("hin")
        exs = alloc([128, 8, 64], F32); Bexs = Buf("exs")
        exsnd = alloc([128, 64], F32); Bexsnd = Buf("exsnd")
        sml = alloc([128, 64], F32); Bsml = Buf("sml")
        identf = alloc([128, 128], F32); identb = alloc([128, 128], BF16); Bid = Buf("ident")
        onesf = alloc([128, 128], F32); Bones = Buf("ones")
        gtb = alloc([128, D], F32); Bgtb = Buf("gtb")
        fgb = alloc([128, D], F32); Bfgb = Buf("fgb")
        NSLOT = 3
        wsl = [alloc([128, 16, 256], BF16) for _ in range(NSLOT)]; Bws = [Buf(f"ws{i}") for i in range(NSLOT)]
        gws = [alloc([128, 4, 128], BF16) for _ in range(2)]; Bgw = [Buf(f"gw{i}") for i in range(2)]
        hnT = alloc([128, 16, T + 6], BF16); BhnT = Buf("hnT")
        v2 = alloc([128, 16, T], F32); Bv2 = [Buf(f"v2_{c}") for c in range(16)]
        vT = alloc([128, 16, T], BF16); BvT = [Buf(f"vT{c}") for c in range(16)]
        hT = alloc([128, 16, T], BF16); BhT = [Buf(f"hT{c}") for c in range(16)]
        yT = alloc([128, 16, T], BF16); ByT = [Buf(f"yT{c}") for c in range(16)]
        dg31 = alloc([128, 31, 128], BF16); Bdg31 = Buf("dg31")
        dg4 = [alloc([128, 8, 128], BF16) for _ in range(2)]; Bdg4 = [Buf(f"dg4{i}") for i in range(2)]
        vpad = [alloc([128, 8, 94], BF16) for _ in range(2)]; Bvpad = [Buf(f"vpad{i}") for i in range(2)]
        xrpad = [alloc([128, T + 6], BF16) for _ in range(2)]; Bxrp = [Buf(f"xrp{i}") for i in range(2)]
        s5t = [alloc([128, 256], F32) for _ in range(2)]; Bs5t = [Buf(f"s5t{i}") for i in range(2)]
        NTMP = 18
        tmpb = alloc([128, NTMP, T], F32)
        Btmp = [Buf(f"tmp{i}") for i in range(NTMP)]
        TM = [tmpb[:, i, :] for i in range(NTMP)]
        xt = tmpb[:, 0:4, :].rearrange("p a b -> p (a b)")
        Bxt = Btmp[0:4]
        xs = [tmpb[:, 4:6, :].rearrange("p a b -> p (a b)").bitcast(BF16)[:, 0:D],
              tmpb[:, 6:8, :].rearrange("p a b -> p (a b)").bitcast(BF16)[:, 0:D]]
        Bxs = [Btmp[4:6], Btmp[6:8]]

        pbank = [es.enter_context(nc.psum_tensor(f"pb{i}", [128, 512], F32))[:, :] for i in range(8)]
        BP = [Buf(f"P{i}") for i in range(8)]

        dsem_cnt = [0]

        def dsem():
            dsem_cnt[0] += 1
            return sem(f"d{dsem_cnt[0]}")

        d_misc = [dsem() for _ in range(4)]
        d_x5 = [dsem() for _ in range(4)]
        d_o = [dsem() for _ in range(4)]
        d_ws = [dsem() for _ in range(NSLOT)]
        d_gw = [dsem() for _ in range(2)]
        d_x = dsem()
        d_ex = dsem()
        d_ab = [[dsem() for _ in range(4)] for _ in range(2)]
        d_ld = [[dsem() for _ in range(4)] for _ in range(4)]
        Babd = [[[Buf(f"abd{t}_{c}_{q}") for q in range(4)] for c in range(16)] for t in range(NT)]
        d_vt = dsem()
        Bvtd = [Buf(f"vtd{t}") for t in range(NT)]
        cc_sem = sem("cc")

        def V(name, c=0, w=1):
            o = VOFF[name] + c
            return vec[:, o:o + w]

        def DVv(name, c=0, w=1):
            o = DVO[name] + c
            return dv[:, o:o + w]

        worder = []
        for b in range(24):
            worder.append(("ada", wada_d[b], None))
        for tl in range(5):
            for c in range(16):
                if tl > 0:
                    worder.append(("ag", win_d[c], None))
                worder.append(("xr", win_d[24 + c // 2], c % 2))
            if tl > 0:
                for c2 in range(8):
                    worder.append(("zc", win_d[16 + c2], None))
        for tl in range(4):
            for c2 in range(8):
                worder.append(("zr", win_d[32 + c2], None))
            for j in range(16):
                worder.append(("gg", win_d[40 + j], None))
                worder.append(("cr", wout_d[j], None))
            for nb in range(8):
                worder.append(("wo", wout_d[16 + nb], None))
        wstate = {"issued": 0, "used": 0}

        def w_issue_upto(n):
            while wstate["issued"] < min(n, len(worder)):
                i = wstate["issued"]
                kind, src, half = worder[i]
                s = i % NSLOT
                srcv = src.rearrange("p (k c) -> p k c", k=16)
                if half is not None:
                    PL.dma(d_ws[s], wsl[s][:, :, 0:128], srcv[:, :, half * 128:(half + 1) * 128], writes=[Bws[s]])
                else:
                    PL.dma(d_ws[s], wsl[s], srcv, writes=[Bws[s]])
                wstate["issued"] += 1

        def wget(kind):
            i = wstate["used"]
            assert worder[i][0] == kind, (worder[i][0], kind, i)
            w_issue_upto(i + NSLOT)
            wstate["used"] += 1
            s = i % NSLOT
            return wsl[s], Bws[s]

        gstate = {"n": 0}

        def gw_load(c):
            s = gstate["n"] % 2
            gstate["n"] += 1
            PL.dma(d_gw[s], gws[s], gw_d[c].rearrange("p (m j) -> p m j", m=4), writes=[Bgw[s]])
            return gws[s], Bgw[s]

        def act(out, in_, func, reads, writes, bias=None, scale=None, accum=None):
            kw = {}
            if bias is not None:
                kw["bias"] = bias
            if scale is not None:
                kw["scale"] = scale
            if accum is not None:
                kw["accum_out"] = accum
            return AC.op(lambda e: e.activation(out=out, in_=in_, func=func, **kw), reads=reads, writes=writes)

        def tt(eng, out, in0, in1, op, reads, writes):
            return eng.op(lambda e: e.tensor_tensor(out=out, in0=in0, in1=in1, op=op), reads=reads, writes=writes)

        def ts(eng, out, in0, s1, s2, op0, op1, reads, writes):
            return eng.op(lambda e: e.tensor_scalar(out=out, in0=in0, scalar1=s1, scalar2=s2, op0=op0, op1=op1),
                          reads=reads, writes=writes)

        def stt(out, in0, scalar, in1, op0, op1, reads, writes):
            return DV.op(lambda e: e.scalar_tensor_tensor(out=out, in0=in0, scalar=scalar, in1=in1, op0=op0, op1=op1),
                         reads=reads, writes=writes)

        def mm(out, lhsT, rhs, start, stop, reads, writes):
            return PE.op(lambda e: e.matmul(out, lhsT=lhsT, rhs=rhs, start=start, stop=stop), reads=reads, writes=writes,
                         inc=bool(stop))

        SY.dma(d_misc[0], vec, vec_d, writes=[Bvec])
        SY.dma(d_misc[1], identf, idn_d, writes=[Bid])
        SY.dma(d_misc[2], fgb, fgb_d, writes=[Bfgb])
        SY.dma(d_misc[3], gtb, bgtb_d, writes=[Bgtb])
        w_issue_upto(NSLOT)
        DV.op(lambda e: e.tensor_copy(out=identb, in_=identf), reads=[Bid], writes=[Bid])
        PL.op(lambda e: e.memset(onesf, 1.0), writes=[Bones])
        for i in range(2):
            PL.op(lambda e, i=i: e.memset(vpad[i], 0.0), writes=[Bvpad[i]])
        ts(DV, DVv("nb_r", 0, 32), V("b_r", 0, 32), -1.0, None, ALU.mult, ALU.bypass, [Bvec], [Bdv])
        ts(DV, DVv("nb_i", 0, 32), V("b_i", 0, 32), -1.0, None, ALU.mult, ALU.bypass, [Bvec], [Bdv])
        ts(DV, DVv("nb_zr", 0, 16), V("b_in", 64, 16), -1.0, None, ALU.mult, ALU.bypass, [Bvec], [Bdv])
        ts(DV, DVv("nb_g", 0, 16), V("b_in", 16, 16), -1.0, None, ALU.mult, ALU.bypass, [Bvec], [Bdv])
        act(DVv("tmp", 0, 32), V("lam", 0, 32), AF.Exp, [Bvec], [Bdv], scale=-1.0)
        act(DVv("tmp", 32, 32), DVv("tmp", 0, 32), AF.Ln, [Bdv], [Bdv], bias=1.0)
        ts(DV, DVv("Lc", 0, 32), DVv("tmp", 32, 32), -8.0, None, ALU.mult, ALU.bypass, [Bdv], [Bdv])
        ts(DV, DVv("L2", 0, 32), DVv("tmp", 32, 32), -16.0, None, ALU.mult, ALU.bypass, [Bdv], [Bdv])
        act(DVv("tmp", 0, 32), V("c", 0, 32), AF.Exp, [Bvec], [Bdv], scale=-1.0)
        ts(DV, DVv("tmp", 0, 32), DVv("tmp", 0, 32), 1.0, None, ALU.add, ALU.bypass, [Bdv], [Bdv])
        DV.op(lambda e: e.reciprocal(out=DVv("tmp", 0, 32), in_=DVv("tmp", 0, 32)), reads=[Bdv], writes=[Bdv])
        tt(DV, DVv("silc", 0, 32), DVv("tmp", 0, 32), V("c", 0, 32), ALU.mult, [Bdv, Bvec], [Bdv])
        scb = TM[8].bitcast(BF16)[:, 0:32].rearrange("p (k n) -> p k n", n=2)
        scbb = tmpb[:, 9:11, :].rearrange("p a b -> p (a b)").bitcast(BF16).rearrange("p (k n) -> p k n", n=128)
        DV.op(lambda e: e.tensor_copy(out=scb[:, :, 0], in_=DVv("silc", 0, 16)), reads=[Bdv], writes=[Btmp[8]])
        DV.op(lambda e: e.tensor_copy(out=scb[:, :, 1], in_=DVv("silc", 16, 16)), reads=[Bdv, Btmp[8]], writes=[Btmp[8]])
        DV.op(lambda e: e.tensor_copy(out=scbb, in_=DVv("silc", 0, 16).unsqueeze(2).broadcast_to([128, 16, 128])),
              reads=[Bdv], writes=[Btmp[9], Btmp[10]])
        pada = pbank[0][:, 0:64].rearrange("p (b n) -> p b n", n=2)
        for b in range(16):
            ws, bw = wget("ada")
            for j in range(2):
                cb = 2 * b + j
                for k in range(16):
                    mm(pada[:, cb, :], ws[:, k, j * 128:(j + 1) * 128], scb[:, k, :], k == 0, k == 15,
                       [bw, Btmp[8]], [BP[0]])
        for b in range(8):
            ws, bw = wget("ada")
            bank = 4 + b // 2
            po = pbank[bank][:, (b % 2) * 256:(b % 2) * 256 + 256]
            for k in range(16):
                mm(po, scbb[:, k, :], ws[:, k, :], k == 0, k == 15, [bw, Btmp[9], Btmp[10]], [BP[bank]])
        for q in range(4):
            tt(DV, gtb[:, q * 512:(q + 1) * 512], pbank[4 + q], gtb[:, q * 512:(q + 1) * 512], ALU.add,
               [BP[4 + q], Bgtb], [Bgtb])
        adaT = DVv("ada", 0, 64).rearrange("p (b n) -> p b n", n=2)
        DV.op(lambda e: e.tensor_copy(out=adaT, in_=pada), reads=[BP[0]], writes=[Bdv])
        for n_, (shn, gscn) in enumerate([("sh", "gsc"), ("shc", "gscc")]):
            tt(DV, DVv(shn, 0, 16), adaT[:, 0:16, n_], V("b_sh", 0, 16), ALU.add, [Bdv, Bvec], [Bdv])
            tt(DV, DVv("tmp", 0, 16), adaT[:, 16:32, n_], V("b_sc", 0, 16), ALU.add, [Bdv, Bvec], [Bdv])
            stt(DVv(gscn, 0, 16), DVv("tmp", 0, 16), 1.0, V("norm_g", 0, 16), ALU.add, ALU.mult, [Bdv, Bvec], [Bdv])

        def make_hn(tile):
            ntok = CTX if tile < 0 else T
            shn, gscn = ("shc", "gscc") if tile < 0 else ("sh", "gsc")
            nblk = ntok // 128
            hview = pbank[2].bitcast(BF16)[:, 0:96].rearrange("p (c t) -> p c t", t=6)
            mview = [pbank[4 + q].bitcast(BF16).rearrange("p (c t) -> p c t", t=256) for q in range(4)]
            blocks = list(range(nblk)) + ["h"]
            for bi, b in enumerate(blocks):
                rows = 6 if b == "h" else 128
                if b == "h":
                    if tile < 0:
                        SY.dma(d_x, xt[0:6, :], ctx_d[0:6, :], writes=Bxt)
                    else:
                        SY.dma(d_x, xt[0:3, :], xp[tile * T:tile * T + 3, :], writes=Bxt)
                        SY.dma(d_x, xt[3:6, :], xp[tile * T + T + 3:tile * T + T + 6, :], writes=Bxt)
                else:
                    src = ctx_d[b * 128:(b + 1) * 128, :] if tile < 0 else xp[tile * T + 3 + b * 128:tile * T + 3 + (b + 1) * 128, :]
                    SY.dma(d_x, xt, src, writes=Bxt)
                xsb = xs[bi % 2]
                bx = Bxs[bi % 2]
                ss = sml[0:rows, bi:bi + 1]
                act(xsb[0:rows, :], xt[0:rows, :], AF.Square, Bxt, bx + [Bsml], accum=ss)
                act(sml[0:rows, 8 + bi:9 + bi], ss, AF.Ln, [Bsml], [Bsml], scale=1.0 / D, bias=EPS)
                act(sml[0:rows, 16 + bi:17 + bi], sml[0:rows, 8 + bi:9 + bi], AF.Exp, [Bsml], [Bsml], scale=-0.5)
                ts(DV, xsb[0:rows, :], xt[0:rows, :], sml[0:rows, 16 + bi:17 + bi], None, ALU.mult, ALU.bypass,
                   Bxt + [Bsml], bx)
                for c in range(16):
                    if b == "h":
                        PE.op(lambda e, c=c, xsb=xsb: e.transpose(out=hview[:, c, :], in_=xsb[0:6, c * 128:(c + 1) * 128],
                                                                   identity=identb[0:6, 0:6]),
                              reads=bx + [Bid], writes=[BP[2]], inc=(c == 15))
                    else:
                        half = b // 2
                        col = (b % 2) * 128
                        PE.op(lambda e, c=c, xsb=xsb, col=col: e.transpose(out=mview[c // 4][:, c % 4, col:col + 128],
                                                                           in_=xsb[:, c * 128:(c + 1) * 128], identity=identb),
                              reads=bx + [Bid], writes=[BP[4 + c // 4]], inc=(c % 4 == 3))
                if b != "h" and b % 2 == 1:
                    half = b // 2
                    for c in range(16):
                        act(hnT[:, c, half * 256:half * 256 + 256], mview[c // 4][:, c % 4, :], AF.Identity,
                            [BP[4 + c // 4], Bdv], [BhnT], bias=DVv(shn, c), scale=DVv(gscn, c))
                if b == "h":
                    for c in range(16):
                        ts(DV, hnT[:, c, ntok:ntok + 3], hview[:, c, 0:3], DVv(gscn, c), DVv(shn, c), ALU.mult, ALU.add,
                           [BP[2], Bdv], [BhnT])
                        ts(DV, hnT[:, c, ntok + 3:ntok + 6], hview[:, c, 3:6], DVv(gscn, c), DVv(shn, c), ALU.mult, ALU.add,
                           [BP[2], Bdv], [BhnT])

        hT32 = hT.rearrange("p a b -> p (a b)").bitcast(F32).rearrange("p (a b) -> p a b", b=T)
        yT32 = yT.rearrange("p a b -> p (a b)").bitcast(F32).rearrange("p (a b) -> p a b", b=T)
        RSET = [[(TM[8 + i], [Btmp[8 + i]]) for i in range(10)],
                [(hT32[:, i, :], [BhT[2 * i], BhT[2 * i + 1]]) for i in range(8)]
                + [(yT32[:, i, :], [ByT[2 * i], ByT[2 * i + 1]]) for i in range(2)]]

        def rnn_xr(tile, c, ws, bw, j):
            ntok = CTX if tile < 0 else T
            wcol = slice(j * 128, (j + 1) * 128)
            pX = pbank[3]; bpX = BP[3]
            pH = pbank[2][:, 0:6]; bpH = BP[2]
            for k in range(16):
                mm(pX[:, 0:ntok], ws[:, k, wcol], hnT[:, k, 0:ntok], k == 0, k == 15, [bw, BhnT], [bpX])
            for k in range(16):
                mm(pH, ws[:, k, wcol], hnT[:, k, ntok:ntok + 6], k == 0, k == 15, [bw, BhnT], [bpH])

        def rnn_xr_evac(tile, c):
            ntok = CTX if tile < 0 else T
            pX = pbank[3]; bpX = BP[3]
            pH = pbank[2][:, 0:6]; bpH = BP[2]
            par = c % 2
            if tile < 0:
                mLv, mRv = V("zero"), V("zero")
            else:
                mLv = V("mL") if tile == 0 else V("one")
                mRv = V("mR") if tile == NT - 1 else V("one")
            b_xr = V("b_in", 48 + c)
            xrp = xrpad[par]; bxrp = Bxrp[par]
            act(xrp[:, 3:3 + ntok], pX[:, 0:ntok], AF.Identity, [bpX, Bvec], [bxrp], bias=b_xr)
            ts(DV, xrp[:, 0:3], pH[:, 0:3], b_xr, mLv, ALU.add, ALU.mult, [bpH, Bvec], [bxrp])
            ts(DV, xrp[:, ntok + 3:ntok + 6], pH[:, 3:6], b_xr, mRv, ALU.add, ALU.mult, [bpH, Bvec], [bxrp])

        def dg4_build(c):
            par = c % 2
            dg = dg4[par]; bdg = Bdg4[par]
            wv = V("rc_w", 0, 128).rearrange("p (d c k) -> p d c k", d=2, c=16)
            for d in range(2):
                PL.op(lambda e, d=d: e.tensor_tensor(out=dg[:, d * 4:(d + 1) * 4, :],
                                                     in0=identf.unsqueeze(1).broadcast_to([128, 4, 128]),
                                                     in1=wv[:, d, c, :].unsqueeze(2).broadcast_to([128, 4, 128]), op=ALU.mult),
                      reads=[Bid, Bvec], writes=[bdg])

        def rnn_p1(tile, c):
            ntok = CTX if tile < 0 else T
            ti = tile + 1
            par = c % 2
            if tile < 0:
                mLv, mRv = V("zero"), V("zero")
            else:
                mLv = V("mL") if tile == 0 else V("one")
                mRv = V("mR") if tile == NT - 1 else V("one")
            b_xr = V("b_in", 48 + c)
            pX = pbank[3]; bpX = BP[3]
            pH = pbank[2][:, 0:6]; bpH = BP[2]
            xrp = xrpad[par]; bxrp = Bxrp[par]
            dg = dg4[par]; bdg = Bdg4[par]
            wv = V("rc_w", 0, 128).rearrange("p (d c k) -> p d c k", d=2, c=16)
            gwt, bgw = gwcache.pop(c)
            S = RSET[par]
            XC = [(S[d * 5][0][:, 0:ntok], S[d * 5][1]) for d in range(2)]
            XCB = [(S[d * 5 + 1][0].bitcast(BF16)[:, 0:ntok], S[d * 5 + 1][1]) for d in range(2)]
            R = [(S[d * 5 + 2][0][:, 0:ntok], S[d * 5 + 2][1]) for d in range(2)]
            II = [(S[d * 5 + 3][0][:, 0:ntok], S[d * 5 + 3][1]) for d in range(2)]
            AA = [(S[d * 5 + 4][0][:, 0:ntok], S[d * 5 + 4][1]) for d in range(2)]
            GB = [(pbank[6], BP[6], pbank[7], BP[7]), (pbank[4], BP[4], pbank[5], BP[5])]
            for d in range(2):
                pXc = pbank[4 + d]; bpXc = BP[4 + d]
                for k in range(4):
                    off = k if d == 0 else 3 + k
                    mm(pXc[:, 0:ntok], dg[:, d * 4 + k, :], xrp[:, off:off + ntok], k == 0, k == 3, [bdg, bxrp], [bpXc])
            for d in range(2):
                pXc = pbank[4 + d]; bpXc = BP[4 + d]
                xc, bxc = XC[d]
                xcb, bxcb = XCB[d]
                cbv = V("rc_b", d * 16 + c)
                act(xc, pXc[:, 0:ntok], AF.Identity, [bpXc, Bvec], bxc, bias=cbv)
                DV.op(lambda e, xcb=xcb, xc=xc: e.tensor_copy(out=xcb, in_=xc), reads=bxc, writes=bxcb)
            for d in range(2):
                pr_, bpr, pi_, bpi = GB[d]
                xcb, bxcb = XCB[d]
                mm(pr_[:, 0:ntok], gwt[:, d, :], xcb, True, True, [bgw] + bxcb, [bpr])
                mm(pi_[:, 0:ntok], gwt[:, 2 + d, :], xcb, True, True, [bgw] + bxcb, [bpi])

        def rnn_p2a(tile, c):
            ntok = CTX if tile < 0 else T
            ti = tile + 1
            par = c % 2
            if tile < 0:
                mLv, mRv = V("zero"), V("zero")
            else:
                mLv = V("mL") if tile == 0 else V("one")
                mRv = V("mR") if tile == NT - 1 else V("one")
            b_xr = V("b_in", 48 + c)
            pX = pbank[3]; bpX = BP[3]
            pH = pbank[2][:, 0:6]; bpH = BP[2]
            xrp = xrpad[par]; bxrp = Bxrp[par]
            dg = dg4[par]; bdg = Bdg4[par]
            wv = V("rc_w", 0, 128).rearrange("p (d c k) -> p d c k", d=2, c=16)
            S = RSET[par]
            XC = [(S[d * 5][0][:, 0:ntok], S[d * 5][1]) for d in range(2)]
            XCB = [(S[d * 5 + 1][0].bitcast(BF16)[:, 0:ntok], S[d * 5 + 1][1]) for d in range(2)]
            R = [(S[d * 5 + 2][0][:, 0:ntok], S[d * 5 + 2][1]) for d in range(2)]
            II = [(S[d * 5 + 3][0][:, 0:ntok], S[d * 5 + 3][1]) for d in range(2)]
            AA = [(S[d * 5 + 4][0][:, 0:ntok], S[d * 5 + 4][1]) for d in range(2)]
            GB = [(pbank[6], BP[6], pbank[7], BP[7]), (pbank[4], BP[4], pbank[5], BP[5])]
            for d in range(2):
                pr_, bpr, pi_, bpi = GB[d]
                act(R[d][0], pr_[:, 0:ntok], AF.Exp, [bpr, Bdv], R[d][1], bias=DVv("nb_r", d * 16 + c), scale=-1.0)
                act(II[d][0], pi_[:, 0:ntok], AF.Exp, [bpi, Bdv], II[d][1], bias=DVv("nb_i", d * 16 + c), scale=-1.0)
            for d in range(2):
                act(R[d][0], R[d][0], AF.Ln, R[d][1], R[d][1], bias=1.0)
            for d in range(2):
                act(R[d][0], R[d][0], AF.Exp, R[d][1], R[d][1], scale=-1.0)
            for d in range(2):
                DV.op(lambda e, d=d: e.reduce_sum(out=sml[:, 32 + 2 * par + d:33 + 2 * par + d], in_=R[d][0], axis=AX.X),
                      reads=R[d][1], writes=[Bsml])
            for d in range(2):
                act(AA[d][0], R[d][0], AF.Exp, R[d][1] + [Bdv], AA[d][1], scale=DVv("Lc", d * 16 + c))
            for d in range(2):
                act(II[d][0], II[d][0], AF.Ln, II[d][1], II[d][1], bias=1.0)

        def rnn_p2b(tile, c):
            ntok = CTX if tile < 0 else T
            ti = tile + 1
            par = c % 2
            if tile < 0:
                mLv, mRv = V("zero"), V("zero")
            else:
                mLv = V("mL") if tile == 0 else V("one")
                mRv = V("mR") if tile == NT - 1 else V("one")
            b_xr = V("b_in", 48 + c)
            pX = pbank[3]; bpX = BP[3]
            pH = pbank[2][:, 0:6]; bpH = BP[2]
            xrp = xrpad[par]; bxrp = Bxrp[par]
            dg = dg4[par]; bdg = Bdg4[par]
            wv = V("rc_w", 0, 128).rearrange("p (d c k) -> p d c k", d=2, c=16)
            S = RSET[par]
            XC = [(S[d * 5][0][:, 0:ntok], S[d * 5][1]) for d in range(2)]
            XCB = [(S[d * 5 + 1][0].bitcast(BF16)[:, 0:ntok], S[d * 5 + 1][1]) for d in range(2)]
            R = [(S[d * 5 + 2][0][:, 0:ntok], S[d * 5 + 2][1]) for d in range(2)]
            II = [(S[d * 5 + 3][0][:, 0:ntok], S[d * 5 + 3][1]) for d in range(2)]
            AA = [(S[d * 5 + 4][0][:, 0:ntok], S[d * 5 + 4][1]) for d in range(2)]
            GB = [(pbank[6], BP[6], pbank[7], BP[7]), (pbank[4], BP[4], pbank[5], BP[5])]
            for d in range(2):
                act(R[d][0], AA[d][0], AF.Square, AA[d][1], R[d][1])
            for d in range(2):
                act(R[d][0], R[d][0], AF.Ln, R[d][1], R[d][1], scale=-1.0, bias=1.0)
            for d in range(2):
                stt(II[d][0], R[d][0], 0.5, II[d][0], ALU.mult, ALU.subtract, R[d][1] + II[d][1], II[d][1])
            for d in range(2):
                act(II[d][0], II[d][0], AF.Exp, II[d][1], II[d][1])
            for d in range(2):
                tt(DV, II[d][0], II[d][0], XC[d][0], ALU.mult, II[d][1] + XC[d][1], II[d][1])
            for d in range(2):
                h = XC[d][0]
                aa, ii = AA[d][0], II[d][0]
                if d == 0:
                    DV.op(lambda e, h=h, aa=aa, ii=ii: e.tensor_tensor_scan(out=h, data0=aa, data1=ii, initial=0.0,
                                                                            op0=ALU.mult, op1=ALU.add),
                          reads=AA[d][1] + II[d][1], writes=XC[d][1])
                else:
                    DV.op(lambda e, h=h, aa=aa, ii=ii: e.tensor_tensor_scan(out=h[:, ::-1], data0=aa[:, ::-1],
                                                                            data1=ii[:, ::-1], initial=0.0,
                                                                            op0=ALU.mult, op1=ALU.add),
                          reads=AA[d][1] + II[d][1], writes=XC[d][1])
            for d in range(2):
                h = XC[d][0]
                base = ((d * 5 + ti) * 2) * 16
                act(tot[:, base + c:base + c + 1], sml[:, 32 + 2 * par + d:33 + 2 * par + d], AF.Exp, [Bsml, Bdv], [Btot],
                    scale=DVv("Lc", d * 16 + c))
                last = h[:, ntok - 1:ntok] if d == 0 else h[:, 0:1]
                DV.op(lambda e, base=base, last=last: e.tensor_copy(out=tot[:, base + 16 + c:base + 16 + c + 1], in_=last),
                      reads=XC[d][1], writes=[Btot])
                if tile >= 0:
                    row = ((tile * 16 + c) * 4 + 2 * d) * 128
                    SY.dma(d_ab[par][2 * d], abd[row:row + 128, :], AA[d][0], reads=AA[d][1], writes=[Babd[tile][c][2 * d]])
                    SY.dma(d_ab[par][2 * d + 1], abd[row + 128:row + 256, :], II[d][0], reads=II[d][1],
                           writes=[Babd[tile][c][2 * d + 1]])


        def rnn_load(tile, c):
            st_ = c % 4
            for q in range(4):
                row = ((tile * 16 + c) * 4 + q) * 128
                SY.dma(d_ld[st_][q], v2[:, st_ * 4 + q, :], abd[row:row + 128, :], reads=[Babd[tile][c][q]],
                       writes=[Bv2[st_ * 4 + q]])

        def rnn_back(tile, c, ws, bw, j):
            par = c % 2
            pZ = pbank[par]; bpZ = BP[par]
            for k in range(16):
                mm(pZ, ws[:, k, j * 128:(j + 1) * 128], hnT[:, k, 0:T], k == 0, k == 15, [bw, BhnT], [bpZ])
            sz = TM[8 + par]; bsz = Btmp[8 + par]
            act(sz, pZ, AF.Silu, [bpZ, Bvec], [bsz], bias=V("b_in", 64 + c))
            st_ = c % 4
            a_f, b_f, a_b, b_b = [v2[:, st_ * 4 + q, :] for q in range(4)]
            Ba_f, Bb_f, Ba_b, Bb_b = [Bv2[st_ * 4 + q] for q in range(4)]
            hf = TM[10 + par]; bhf = Btmp[10 + par]
            hb = TM[12 + par]; bhb = Btmp[12 + par]
            inf = hin[:, (0 * 4 + tile) * 16 + c:(0 * 4 + tile) * 16 + c + 1]
            inb = hin[:, (1 * 4 + tile) * 16 + c:(1 * 4 + tile) * 16 + c + 1]
            DV.op(lambda e: e.tensor_tensor_scan(out=hf, data0=a_f, data1=b_f, initial=inf, op0=ALU.mult, op1=ALU.add),
                  reads=[Ba_f, Bb_f, Bhin], writes=[bhf])
            DV.op(lambda e: e.tensor_tensor_scan(out=hb[:, ::-1], data0=a_b[:, ::-1], data1=b_b[:, ::-1], initial=inb,
                                                 op0=ALU.mult, op1=ALU.add),
                  reads=[Ba_b, Bb_b, Bhin], writes=[bhb])
            tt(PL, hf, hf, hb, ALU.add, [bhf, bhb], [bhf])
            tt(DV, hT[:, c, :], hf, sz, ALU.mult, [bhf, bsz], [BhT[c]])

        cw = V("conv_w", 0, 496).rearrange("p (c k) -> p c k", k=31)
        s1 = TM[0]; s2 = TM[1]; bs1 = Btmp[0]; bs2 = Btmp[1]
        mean = TM[6]; bmean = Btmp[6]
        rstd = TM[7]; brstd = Btmp[7]

        def conv_ag(c, ws, bw):
            pa, bpa = pbank[0], BP[0]
            pg, bpg = pbank[1], BP[1]
            for k in range(16):
                mm(pa, ws[:, k, 0:128], hnT[:, k, 0:T], k == 0, k == 15, [bw, BhnT], [bpa])
            for k in range(16):
                mm(pg, ws[:, k, 128:256], hnT[:, k, 0:T], k == 0, k == 15, [bw, BhnT], [bpg])

        def conv_front(c):
            par = c % 2
            pa, bpa = pbank[0], BP[0]
            pg, bpg = pbank[1], BP[1]
            sg = TM[2 + par]; bsg = Btmp[2 + par]
            act(sg, pg, AF.Exp, [bpg, Bdv], [bsg], bias=DVv("nb_g", c), scale=-1.0)
            act(sg, sg, AF.Ln, [bsg], [bsg], bias=1.0)
            act(sg, sg, AF.Exp, [bsg], [bsg], scale=-1.0)
            vp = vpad[par]; bvp = Bvpad[par]
            stt(vp[:, :, 15:79], pa.rearrange("p (r t) -> p r t", t=64), V("b_in", c),
                sg.rearrange("p (r t) -> p r t", t=64), ALU.add, ALU.mult, [bpa, bsg, Bvec], [bvp])

        def dg31_build(c):
            DV.op(lambda e: e.tensor_tensor(out=dg31, in0=identf.unsqueeze(1).broadcast_to([128, 31, 128]),
                                            in1=cw[:, c, :].unsqueeze(2).broadcast_to([128, 31, 128]), op=ALU.mult),
                  reads=[Bid, Bvec], writes=[Bdg31])

        def conv_mm(c):
            par = c % 2
            vp = vpad[par]; bvp = Bvpad[par]
            pc, bpc = pbank[2], BP[2]
            for k in range(31):
                mm(pc.rearrange("p (r t) -> p r t", t=64), dg31[:, k, :], vp[:, :, k:k + 64], k == 0, k == 30,
                   [Bdg31, bvp], [bpc])
            act(v2[:, c, :], pc, AF.Identity, [bpc, Bvec], [Bv2[c]], bias=V("conv_b", c))
            sq = TM[4 + par]; bsq = Btmp[4 + par]
            act(sq, pc, AF.Square, [bpc, Bvec], [bsq], bias=V("conv_b", c))
            if c == 0:
                PL.op(lambda e: e.tensor_copy(out=s1, in_=v2[:, 0, :]), reads=[Bv2[0]], writes=[bs1])
                PL.op(lambda e: e.tensor_copy(out=s2, in_=sq), reads=[bsq], writes=[bs2])
            else:
                tt(PL, s1, s1, v2[:, c, :], ALU.add, [bs1, Bv2[c]], [bs1])
                tt(PL, s2, s2, sq, ALU.add, [bs2, bsq], [bs2])

        def ln_stats():
            mm(pbank[0], onesf, s1, True, True, [Bones, bs1], [BP[0]])
            mm(pbank[1], onesf, s2, True, True, [Bones, bs2], [BP[1]])
            ts(DV, mean, pbank[0], 1.0 / D, None, ALU.mult, ALU.bypass, [BP[0]], [bmean])
            ts(DV, rstd, pbank[1], 1.0 / D, None, ALU.mult, ALU.bypass, [BP[1]], [brstd])
            msq = TM[4]; bmsq = Btmp[4]
            act(msq, mean, AF.Square, [bmean], [bmsq])
            tt(DV, rstd, rstd, msq, ALU.subtract, [brstd, bmsq], [brstd])
            act(rstd, rstd, AF.Ln, [brstd], [brstd], bias=EPS)
            act(rstd, rstd, AF.Exp, [brstd], [brstd], scale=-0.5)

        def conv_back(c, ws, bw, j):
            par = c % 2
            pz, bpz = pbank[par], BP[par]
            for k in range(16):
                mm(pz, ws[:, k, j * 128:(j + 1) * 128], hnT[:, k, 0:T], k == 0, k == 15, [bw, BhnT], [bpz])
            sz = TM[2 + par]; bsz = Btmp[2 + par]
            act(sz, pz, AF.Silu, [bpz, Bvec], [bsz], bias=V("b_in", 32 + c))
            t1 = TM[4 + par]; bt1 = Btmp[4 + par]
            tt(PL, t1, v2[:, c, :], mean, ALU.subtract, [Bv2[c], bmean], [bt1])
            tt(DV, t1, t1, rstd, ALU.mult, [bt1, brstd], [bt1])
            act(t1, t1, AF.Silu, [bt1, Bvec], [bt1], bias=V("ln_b", c), scale=V("ln_g", c))
            tt(DV, vT[:, c, :], t1, sz, ALU.mult, [bt1, bsz], [BvT[c]])

        gwcache = {}

        def chunk_head(tile, c):
            gwcache[c] = gw_load(c)
            dg4_build(c)
            if tile >= 0:
                if c == 0:
                    dg31_build(0)
                wsa, bwa = wget("ag")
                conv_ag(c, wsa, bwa)
            wsx, bwx = wget("xr")
            rnn_xr(tile, c, wsx, bwx, 0)
            if tile >= 0:
                conv_front(c)
            rnn_xr_evac(tile, c)

        def chunk_mid(tile, c):
            if tile >= 0:
                conv_mm(c)
            rnn_p1(tile, c)
            if tile >= 0 and c + 1 < 16:
                dg31_build(c + 1)

        for tile in [-1, 0, 1, 2, 3]:
            make_hn(tile)
            chunk_head(tile, 0)
            chunk_mid(tile, 0)
            for c in range(16):
                if c + 1 < 16:
                    chunk_head(tile, c + 1)
                rnn_p2a(tile, c)
                if c + 1 < 16:
                    chunk_mid(tile, c + 1)
                rnn_p2b(tile, c)
            if tile >= 0:
                ln_stats()
                for c2 in range(8):
                    ws, bw = wget("zc")
                    for j in range(2):
                        conv_back(2 * c2 + j, ws, bw, j)
                SY.dma(d_vt, vtd[tile * 128:(tile + 1) * 128, :], vT.rearrange("p a b -> p (a b)"), reads=BvT, writes=[Bvtd[tile]])


        def TOT(d, ti, ab, w=16):
            o = ((d * 5 + ti) * 2 + ab) * 16
            return tot[:, o:o + w]

        def small(eng, fn, reads, writes):
            return eng.op(fn, reads=reads, writes=writes)

        for d in range(2):
            Aacc = exsnd[:, d * 32:d * 32 + 16]
            Bacc = exsnd[:, d * 32 + 16:d * 32 + 32]
            order = [1, 2, 3, 4] if d == 0 else [4, 3, 2, 1]
            first = order[0]
            DV.op(lambda e, Aacc=Aacc, d=d, first=first: e.tensor_copy(out=Aacc, in_=TOT(d, first, 0)), reads=[Btot], writes=[Bexsnd])
            DV.op(lambda e, Bacc=Bacc, d=d, first=first: e.tensor_copy(out=Bacc, in_=TOT(d, first, 1)), reads=[Btot, Bexsnd], writes=[Bexsnd])
            for ti in order[1:]:
                tt(DV, Bacc, Bacc, TOT(d, ti, 0), ALU.mult, [Bexsnd, Btot], [Bexsnd])
                tt(DV, Bacc, Bacc, TOT(d, ti, 1), ALU.add, [Bexsnd, Btot], [Bexsnd])
                tt(DV, Aacc, Aacc, TOT(d, ti, 0), ALU.mult, [Bexsnd, Btot], [Bexsnd])
        Bexd = Buf("exdram")
        PL.dma(d_ex, exi.ap(), exsnd, reads=[Bexsnd], writes=[Bexd])
        PL.deps(reads=[Bexd])
        cc_sem.count += 1
        PL.prog.append(("ins", lambda e: e.collective_compute("AllGather", ALU.bypass, replica_groups=[list(range(NCORES))],
                                                             ins=[exi.ap().opt()], outs=[exo.ap().opt()]), cc_sem, 1))
        Bexd.last_write = (cc_sem, cc_sem.count, "cc")
        Bexd.reads = {}
        PL.dma(d_ex, exs, exo.ap().rearrange("(r p) f -> p r f", p=128), reads=[Bexd], writes=[Bexs])
        hcur = DVv("tmp", 0, 16)
        tmpv = DVv("tmp", 16, 16)
        for d in range(2):
            DV.op(lambda e, d=d: e.tensor_copy(out=hcur, in_=TOT(d, 0, 1)), reads=[Btot, Bdv], writes=[Bdv])
            cores = range(NCORES) if d == 0 else range(NCORES - 1, -1, -1)
            mname = "fm" if d == 0 else "bm"
            for j in cores:
                Aj = exs[:, j, d * 32:d * 32 + 16]
                Bj = exs[:, j, d * 32 + 16:d * 32 + 32]
                tt(DV, tmpv, hcur, Aj, ALU.mult, [Bdv, Bexs], [Bdv])
                tt(DV, tmpv, tmpv, Bj, ALU.add, [Bdv, Bexs], [Bdv])
                tt(DV, tmpv, tmpv, hcur, ALU.subtract, [Bdv], [Bdv])
                stt(hcur, tmpv, V(mname, j), hcur, ALU.mult, ALU.add, [Bdv, Bvec], [Bdv])
            tiles = [0, 1, 2, 3] if d == 0 else [3, 2, 1, 0]
            for n_, tl in enumerate(tiles):
                dst = hin[:, (d * 4 + tl) * 16:(d * 4 + tl) * 16 + 16]
                if n_ == 0:
                    DV.op(lambda e, dst=dst: e.tensor_copy(out=dst, in_=hcur), reads=[Bdv, Bhin], writes=[Bhin])
                else:
                    prev = tiles[n_ - 1]
                    src = hin[:, (d * 4 + prev) * 16:(d * 4 + prev) * 16 + 16]
                    tt(DV, dst, src, TOT(d, prev + 1, 0), ALU.mult, [Bhin, Btot], [Bhin])
                    tt(DV, dst, dst, TOT(d, prev + 1, 1), ALU.add, [Bhin, Btot], [Bhin])

        for tile in range(NT):
            make_hn(tile)
            SY.dma(d_vt, vT.rearrange("p a b -> p (a b)"), vtd[tile * 128:(tile + 1) * 128, :], reads=[Bvtd[tile]], writes=BvT)
            for c in range(3):
                rnn_load(tile, c)
            for c2 in range(8):
                ws, bw = wget("zr")
                for j in range(2):
                    c = 2 * c2 + j
                    if c + 3 < 16:
                        rnn_load(tile, c + 3)
                    rnn_back(tile, c, ws, bw, j)
            for j in range(16):
                ws, bw = wget("gg")
                par = j % 2
                pgc, bpgc = pbank[4 * par], BP[4 * par]
                pgr, bpgr = pbank[4 * par + 1], BP[4 * par + 1]
                pyc, bpyc = pbank[4 * par + 2], BP[4 * par + 2]
                pyr, bpyr = pbank[4 * par + 3], BP[4 * par + 3]
                for k in range(16):
                    mm(pgc, ws[:, k, 0:128], hnT[:, k, 0:T], k == 0, k == 15, [bw, BhnT], [bpgc])
                for k in range(16):
                    mm(pgr, ws[:, k, 128:256], hnT[:, k, 0:T], k == 0, k == 15, [bw, BhnT], [bpgr])
                ws2, bw2 = wget("cr")
                for k in range(16):
                    mm(pyc, ws2[:, k, 0:128], vT[:, k, :], k == 0, k == 15, [bw2, BvT[k]], [bpyc])
                for k in range(16):
                    mm(pyr, ws2[:, k, 128:256], hT[:, k, :], k == 0, k == 15, [bw2, BhT[k]], [bpyr])
                sgc = TM[10 + par]; bsgc = Btmp[10 + par]
                sgr = TM[12 + par]; bsgr = Btmp[12 + par]
                act(sgc, pgc, AF.Sigmoid, [bpgc, Bvec], [bsgc], bias=V("b_in", 80 + j))
                act(sgr, pgr, AF.Sigmoid, [bpgr, Bvec], [bsgr], bias=V("b_in", 96 + j))
                tt(DV, sgc, pyc, sgc, ALU.mult, [bpyc, bsgc], [bsgc])
                tt(DV, sgr, pyr, sgr, ALU.mult, [bpyr, bsgr], [bsgr])
                tt(PL, yT[:, j, :], sgc, sgr, ALU.add, [bsgc, bsgr], [ByT[j]])
            xn = v2.rearrange("p a b -> p (a b)").rearrange("p (t n) -> p t n", t=4)
            for tb in range(4):
                r0 = tile * T + 3 + tb * 128
                SY.dma(d_x5[tb], xn[:, tb, :], xp[r0:r0 + 128, :], writes=Bv2[4 * tb:4 * tb + 4])
            for nb in range(8):
                ws, bw = wget("wo")
                for tb in range(4):
                    par = (nb * 4 + tb) % 2
                    po, bpo = pbank[par][:, 0:256], BP[par]
                    for k in range(16):
                        mm(po, yT[:, k, tb * 128:(tb + 1) * 128], ws[:, k, :], k == 0, k == 15, [ByT[k], bw], [bpo])
                    tmp5 = s5t[par]; bt5 = Bs5t[par]
                    tt(DV, tmp5, po, gtb[:, nb * 256:(nb + 1) * 256], ALU.mult, [bpo, Bgtb], [bt5])
                    bx = Bv2[4 * tb + nb // 2]
                    tt(PL, xn[:, tb, nb * 256:(nb + 1) * 256], xn[:, tb, nb * 256:(nb + 1) * 256], tmp5, ALU.add,
                       [bx, bt5], [bx])
            for tb in range(4):
                bxs_ = Bv2[4 * tb:4 * tb + 4]
                junk = tmpb[:, 4:6, :].rearrange("p a b -> p (a b)").bitcast(BF16)[:, 0:D]
                act(junk, xn[:, tb, :], AF.Square, bxs_, Btmp[4:6] + [Bsml], accum=sml[:, 40 + tb:41 + tb])
                act(sml[:, 44 + tb:45 + tb], sml[:, 40 + tb:41 + tb], AF.Ln, [Bsml], [Bsml], scale=1.0 / D, bias=EPS)
                act(sml[:, 48 + tb:49 + tb], sml[:, 44 + tb:45 + tb], AF.Exp, [Bsml], [Bsml], scale=-0.5)
                stt(xn[:, tb, :], xn[:, tb, :], sml[:, 48 + tb:49 + tb], fgb, ALU.mult, ALU.mult, bxs_ + [Bsml, Bfgb], bxs_)
                r0 = tile * T + tb * 128
                SY.dma(d_o[tb], out_d[r0:r0 + 128, :], xn[:, tb, :], reads=bxs_)

        for tb in range(4):
            SY.wait_tok((d_o[tb], d_o[tb].count, "dma"))
        assert wstate["used"] == len(worder), (wstate, len(worder))

        _DEBUG["engs"] = E
        with nc.Block() as block:
            @block.sync
            def _(e):
                SY.replay(e)

            @block.scalar
            def _(e):
                AC.replay(e)

            @block.tensor
            def _(e):
                PE.replay(e)

            @block.vector
            def _(e):
                DV.replay(e)

            @block.gpsimd
            def _(e):
                PL.replay(e)
    return nc


_CACHE = {}


def kernel(x, c, ctx, c_ctx, w_ada, b_ada, norm_g, w_in, b_in, conv_dw, conv_dw_b, conv_ln_g, conv_ln_b, w_conv_out,
           rnn_conv, rnn_conv_b, rnn_w_r, rnn_b_r, rnn_w_i, rnn_b_i, rnn_lam, w_rnn_out, w_o, final_g):
    f = lambda a: np.asarray(a, np.float32)
    x = f(x)[0]
    ctx = f(ctx)[0]
    w_ada, b_ada, norm_g, w_in, b_in = f(w_ada)[0], f(b_ada)[0], f(norm_g)[0], f(w_in)[0], f(b_in)[0]
    conv_dw, conv_dw_b, conv_ln_g, conv_ln_b = f(conv_dw)[0], f(conv_dw_b)[0], f(conv_ln_g)[0], f(conv_ln_b)[0]
    w_conv_out, rnn_conv, rnn_conv_b = f(w_conv_out)[0], f(rnn_conv)[0], f(rnn_conv_b)[0]
    rnn_w_r, rnn_b_r, rnn_w_i, rnn_b_i, rnn_lam = f(rnn_w_r)[0], f(rnn_b_r)[0], f(rnn_w_i)[0], f(rnn_b_i)[0], f(rnn_lam)[0]
    w_rnn_out, w_o, final_g = f(w_rnn_out)[0], f(w_o)[0], f(final_g)

    ar = np.arange
    ada_cols = [ar(b * 256, (b + 1) * 256) for b in range(24)]
    wada_b = _blocks(w_ada, ada_cols)
    G = D
    in_cols = []
    for cc in range(16):
        in_cols.append(np.concatenate([ar(cc * 128, cc * 128 + 128), G + ar(cc * 128, cc * 128 + 128)]))
    for c2 in range(8):
        in_cols.append(2 * G + ar(c2 * 256, c2 * 256 + 256))
    for c2 in range(8):
        in_cols.append(3 * G + ar(c2 * 256, c2 * 256 + 256))
    for c2 in range(8):
        in_cols.append(4 * G + ar(c2 * 256, c2 * 256 + 256))
    for j in range(16):
        in_cols.append(np.concatenate([5 * G + ar(j * 128, j * 128 + 128), 6 * G + ar(j * 128, j * 128 + 128)]))
    win_b = _blocks(w_in, in_cols)
    wcr = np.concatenate([w_conv_out, w_rnn_out], axis=1)
    out_cols = [np.concatenate([ar(j * 128, j * 128 + 128), D + ar(j * 128, j * 128 + 128)]) for j in range(16)]
    wout_b = np.concatenate([_blocks(wcr, out_cols), _blocks(w_o, [ar(nb * 256, nb * 256 + 256) for nb in range(8)])], axis=0)
    gw = np.empty((16, 128, 4, 128), np.float32)
    for h in range(16):
        gw[h, :, 0, :] = rnn_w_r[0, h]
        gw[h, :, 1, :] = rnn_w_r[1, h]
        gw[h, :, 2, :] = rnn_w_i[0, h]
        gw[h, :, 3, :] = rnn_w_i[1, h]
    gw = gw.reshape(16, 128, 512)

    def make_vec(core):
        v = np.zeros((128, NV), np.float32)

        def put(name, arr):
            arr = np.asarray(arr, np.float32)
            v[:, VOFF[name]:VOFF[name] + arr.shape[1]] = arr
        put("c", _fm(f(c)[0]))
        put("cctx", _fm(f(c_ctx)))
        put("norm_g", _fm(norm_g))
        put("b_sh", _fm(b_ada[0:D]))
        put("b_sc", _fm(b_ada[D:2 * D]))
        put("b_in", _fm(b_in))
        put("conv_b", _fm(conv_dw_b))
        put("ln_g", _fm(conv_ln_g))
        put("ln_b", _fm(conv_ln_b))
        put("conv_w", conv_dw.T.reshape(16, 128, 31).transpose(1, 0, 2).reshape(128, 496))
        put("rc_w", rnn_conv.transpose(2, 0, 1).reshape(16, 128, 2, 4).transpose(1, 2, 0, 3).reshape(128, 128))
        put("rc_b", np.concatenate([_fm(rnn_conv_b[0]), _fm(rnn_conv_b[1])], axis=1))
        put("b_r", np.concatenate([_fm(rnn_b_r[0]), _fm(rnn_b_r[1])], axis=1))
        put("b_i", np.concatenate([_fm(rnn_b_i[0]), _fm(rnn_b_i[1])], axis=1))
        put("lam", np.concatenate([_fm(rnn_lam[0]), _fm(rnn_lam[1])], axis=1))
        put("mL", np.full((128, 1), 0.0 if core == 0 else 1.0))
        put("mR", np.full((128, 1), 0.0 if core == NCORES - 1 else 1.0))
        put("fm", np.tile((np.arange(8) < core).astype(np.float32)[None, :], (128, 1)))
        put("bm", np.tile((np.arange(8) > core).astype(np.float32)[None, :], (128, 1)))
        put("one", np.ones((128, 1)))
        return v

    fgb = np.ascontiguousarray(np.broadcast_to(final_g[None, :], (128, D)))
    bgtb = np.ascontiguousarray(np.broadcast_to(b_ada[None, 2 * D:3 * D], (128, D)))
    idn = np.eye(128, dtype=np.float32)
    in_maps = []
    for k in range(NCORES):
        xpk = np.zeros((TOK + 6, D), np.float32)
        lo = k * TOK - 3
        hi = k * TOK + TOK + 3
        s0 = max(lo, 0)
        s1 = min(hi, x.shape[0])
        xpk[s0 - lo:s1 - lo] = x[s0:s1]
        in_maps.append({"xp": xpk, "ctx": ctx, "vec": make_vec(k), "idn": idn, "fgb": fgb, "bgtb": bgtb,
                        "wada": wada_b, "win": win_b, "wout": wout_b, "gw": gw})
    if "nc" not in _CACHE:
        _CACHE["nc"] = build_program()
    res = run_bass_kernel_spmd(_CACHE["nc"], in_maps, core_ids=list(range(NCORES)))
    out = np.concatenate([r["out"] for r in res.results], axis=0)
    return out[None].astype(np.float32)
```
